# Optimizing a Trainium2 kernel written in Bass

```python
import jax, jax.numpy as jnp
from jax import lax
import numpy as np

D_MODEL = 1024
BATCH = 32
SEQ = 256
DEPTH = 1
DEC_BATCH = 4
DEC_SEQ = 4096
PAST_LEN = 512

GRID_W = 64
MIX_WIDTH = D_MODEL
POOL_WIDTH = MIX_WIDTH // 2
LRU_WIDTH = MIX_WIDTH - POOL_WIDTH
IN_WIDTH = POOL_WIDTH + 2 * LRU_WIDTH
POOL_WINDOWS = (2, 4, 8, 16)
POOL_GROUPS = len(POOL_WINDOWS)
POOL_GROUP_DIM = POOL_WIDTH // POOL_GROUPS
LRU_HEADS = 8
LRU_HEAD_DIM = LRU_WIDTH // LRU_HEADS
CONV_WIDTH = 4
CONV_LEFT = 2
LRU_C = 8.0
N_DIRS = 2
D_FF = ((8 * D_MODEL + 3 * 256 - 1) // (3 * 256)) * 256
N_MOD = 6
NORM_EPS = 1e-6

kernel_name = 'hymba_pool_rglru_prefix_dit_step'


def rmsnorm(x, g):
    xf = x.astype(jnp.float32)
    y = xf * lax.rsqrt(jnp.mean(xf * xf, axis=-1, keepdims=True) + NORM_EPS) * g.astype(jnp.float32)
    return y.astype(x.dtype)


def modulation(cvec, w_mod, b_mod):
    m = jax.nn.silu(cvec) @ w_mod + b_mod
    return jnp.split(m[..., None, :], N_MOD, axis=-1)


def window_mean_minus_self(x, w):
    n, L, ch = x.shape
    cs = jnp.concatenate([jnp.zeros((n, 1, ch), x.dtype), jnp.cumsum(x, axis=1)], axis=1)
    t = jnp.arange(L)
    left = w // 2
    right = w - 1 - left
    lo = jnp.clip(t - left, 0, L)
    hi = jnp.clip(t + right + 1, 0, L)
    s = cs[:, hi] - cs[:, lo]
    cnt = (hi - lo).astype(x.dtype)[None, :, None]
    return s / cnt - x


def pool_mixer(u, w_pool, pool_scale, grid):
    b, L, _ = u.shape
    uf = u.astype(jnp.float32)
    if grid:
        rows = L // GRID_W
        uf = uf.reshape(b * rows, GRID_W, POOL_WIDTH)
    groups = [window_mean_minus_self(uf[..., g * POOL_GROUP_DIM:(g + 1) * POOL_GROUP_DIM], w)
              for g, w in enumerate(POOL_WINDOWS)]
    m = jnp.stack(groups, axis=-2).reshape(b, L, POOL_GROUPS, POOL_GROUP_DIM)
    y = jnp.einsum('blgc,gcd->blgd', m, w_pool.astype(jnp.float32)).reshape(b, L, POOL_WIDTH)
    return (y * pool_scale.astype(jnp.float32)).astype(u.dtype)


def centred_dwconv(x, w, bias):
    L = x.shape[1]
    xp = jnp.pad(x, ((0, 0), (CONV_LEFT, CONV_WIDTH - 1 - CONV_LEFT), (0, 0)))
    wf = w.astype(jnp.float32)
    out = xp[:, 0:L] * wf[0]
    for k in range(1, CONV_WIDTH):
        out = out + xp[:, k:k + L] * wf[k]
    return out + bias.astype(jnp.float32)


def linear_scan(a, bterm, h0, reverse):
    if reverse:
        bterm = bterm.at[:, -1].add(a[:, -1] * h0)
    else:
        bterm = bterm.at[:, 0].add(a[:, 0] * h0)

    def combine(p, q):
        a1, b1 = p
        a2, b2 = q
        return a1 * a2, a2 * b1 + b2

    _, h = lax.associative_scan(combine, (a, bterm), reverse=reverse, axis=1)
    return h


def rg_lru_bidir(xb, w_a, b_a, w_x, b_x, lam, h0):
    b, L, _ = xb.shape
    xh = xb.reshape(b, L, LRU_HEADS, LRU_HEAD_DIM)
    y = None
    finals = []
    for d in range(N_DIRS):
        r = jax.nn.sigmoid(jnp.einsum('blhi,hij->blhj', xh, w_a[d].astype(jnp.float32)).reshape(b, L, LRU_WIDTH)
                           + b_a[d].astype(jnp.float32))
        i = jax.nn.sigmoid(jnp.einsum('blhi,hij->blhj', xh, w_x[d].astype(jnp.float32)).reshape(b, L, LRU_WIDTH)
                           + b_x[d].astype(jnp.float32))
        log_a = LRU_C * r * jax.nn.log_sigmoid(lam[d].astype(jnp.float32))
        a = jnp.exp(log_a)
        mult = jnp.sqrt(jnp.maximum(-jnp.expm1(2.0 * log_a), 0.0))
        h = linear_scan(a, mult * (i * xb), h0[:, d], reverse=(d == 1))
        finals.append(h[:, -1] if d == 0 else h[:, 0])
        y = h if y is None else y + h
    return y, jnp.stack(finals, axis=1)


def trunk_layer(x, cvec, h0, grid, p):
    shift1, scale1, gate1, shift2, scale2, gate2 = modulation(cvec, p['w_mod'], p['b_mod'])
    h = rmsnorm(x, p['norm_mix_pre']) * (1.0 + scale1) + shift1
    u = h @ p['w_in']
    u_pool = u[..., :POOL_WIDTH]
    u_x = u[..., POOL_WIDTH:POOL_WIDTH + LRU_WIDTH]
    u_gate = u[..., POOL_WIDTH + LRU_WIDTH:]
    y_pool = pool_mixer(u_pool, p['w_pool'], p['pool_scale'], grid)
    xb = centred_dwconv(u_x.astype(jnp.float32), p['conv_w'], p['conv_b'])
    y_lru, h_final = rg_lru_bidir(xb, p['lru_w_a'], p['lru_b_a'], p['lru_w_x'], p['lru_b_x'],
                                  p['lru_lambda'], h0)
    y_lru = (y_lru * jax.nn.gelu(u_gate.astype(jnp.float32))).astype(x.dtype)
    mix = jnp.concatenate([y_pool, y_lru], axis=-1) @ p['w_out']
    x = x + gate1 * rmsnorm(mix, p['norm_mix_post'])
    h2 = rmsnorm(x, p['norm_ffn_pre']) * (1.0 + scale2) + shift2
    gu = h2 @ p['w_ffn_in']
    ff = (jax.nn.silu(gu[..., :D_FF]) * gu[..., D_FF:]) @ p['w_ffn_out']
    x = x + gate2 * rmsnorm(ff, p['norm_ffn_post'])
    return x, h_final


def setup_inputs(seed: int = 0) -> dict:
    key = jax.random.key(seed)
    ks = jax.random.split(key, 24)
    f32 = jnp.float32
    nrm = lambda k, shape, s: jax.random.normal(k, shape, f32) * s
    a0 = jax.random.uniform(ks[20], (DEPTH, N_DIRS, LRU_WIDTH), f32, minval=0.9, maxval=0.999)
    s0 = a0 ** (1.0 / LRU_C)
    lam = jnp.log(s0) - jnp.log1p(-s0)
    return {
        'x_prompt': nrm(ks[0], (BATCH, SEQ, D_MODEL), 1.0),
        'x_sample': nrm(ks[1], (DEC_BATCH, DEC_SEQ, D_MODEL), 1.0),
        'c': nrm(ks[2], (DEC_BATCH, D_MODEL), 1.0),
        'state_lru': nrm(ks[3], (DEC_BATCH, DEPTH, N_DIRS, LRU_WIDTH), 0.5),
        'c_ctx': nrm(ks[4], (D_MODEL,), 1.0),
        'w_mod': nrm(ks[5], (DEPTH, D_MODEL, N_MOD * D_MODEL), D_MODEL ** -0.5),
        'b_mod': nrm(ks[6], (DEPTH, N_MOD * D_MODEL), 0.01),
        'norm_mix_pre': 1.0 + nrm(ks[7], (DEPTH, D_MODEL), 0.05),
        'norm_mix_post': 1.0 + nrm(ks[8], (DEPTH, D_MODEL), 0.05),
        'norm_ffn_pre': 1.0 + nrm(ks[9], (DEPTH, D_MODEL), 0.05),
        'norm_ffn_post': 1.0 + nrm(ks[10], (DEPTH, D_MODEL), 0.05),
        'w_in': nrm(ks[11], (DEPTH, D_MODEL, IN_WIDTH), D_MODEL ** -0.5),
        'w_pool': nrm(ks[12], (DEPTH, POOL_GROUPS, POOL_GROUP_DIM, POOL_GROUP_DIM), POOL_GROUP_DIM ** -0.5),
        'pool_scale': 1.0 + nrm(ks[13], (DEPTH, POOL_WIDTH), 0.1),
        'conv_w': nrm(ks[14], (DEPTH, CONV_WIDTH, LRU_WIDTH), CONV_WIDTH ** -0.5),
        'conv_b': nrm(ks[15], (DEPTH, LRU_WIDTH), 0.01),
        'lru_w_a': nrm(ks[16], (DEPTH, N_DIRS, LRU_HEADS, LRU_HEAD_DIM, LRU_HEAD_DIM), LRU_HEAD_DIM ** -0.5),
        'lru_b_a': nrm(ks[17], (DEPTH, N_DIRS, LRU_WIDTH), 0.01),
        'lru_w_x': nrm(ks[18], (DEPTH, N_DIRS, LRU_HEADS, LRU_HEAD_DIM, LRU_HEAD_DIM), LRU_HEAD_DIM ** -0.5),
        'lru_b_x': nrm(ks[19], (DEPTH, N_DIRS, LRU_WIDTH), 0.01),
        'lru_lambda': lam,
        'w_out': nrm(ks[21], (DEPTH, MIX_WIDTH, D_MODEL), MIX_WIDTH ** -0.5),
        'w_ffn_in': nrm(ks[22], (DEPTH, D_MODEL, 2 * D_FF), D_MODEL ** -0.5),
        'w_ffn_out': nrm(ks[23], (DEPTH, D_FF, D_MODEL), D_FF ** -0.5),
    }


def reference(x_prompt, x_sample, c, state_lru, c_ctx, w_mod, b_mod, norm_mix_pre, norm_mix_post,
              norm_ffn_pre, norm_ffn_post, w_in, w_pool, pool_scale, conv_w, conv_b,
              lru_w_a, lru_b_a, lru_w_x, lru_b_x, lru_lambda, w_out, w_ffn_in, w_ffn_out):
    y_prompt = x_prompt
    y_sample = x_sample
    h0_ctx = jnp.zeros((x_prompt.shape[0], N_DIRS, LRU_WIDTH), jnp.float32)
    ctx_states = []
    for l in range(DEPTH):
        p = {
            'w_mod': w_mod[l], 'b_mod': b_mod[l],
            'norm_mix_pre': norm_mix_pre[l], 'norm_mix_post': norm_mix_post[l],
            'norm_ffn_pre': norm_ffn_pre[l], 'norm_ffn_post': norm_ffn_post[l],
            'w_in': w_in[l], 'w_pool': w_pool[l], 'pool_scale': pool_scale[l],
            'conv_w': conv_w[l], 'conv_b': conv_b[l],
            'lru_w_a': lru_w_a[l], 'lru_b_a': lru_b_a[l],
            'lru_w_x': lru_w_x[l], 'lru_b_x': lru_b_x[l], 'lru_lambda': lru_lambda[l],
            'w_out': w_out[l], 'w_ffn_in': w_ffn_in[l], 'w_ffn_out': w_ffn_out[l],
        }
        y_prompt, h_ctx = trunk_layer(y_prompt, c_ctx, h0_ctx, False, p)
        ctx_states.append(h_ctx)
        y_sample, _ = trunk_layer(y_sample, c, state_lru[:, l].astype(jnp.float32), True, p)
    new_state_lru = jnp.stack(ctx_states, axis=1).astype(x_prompt.dtype)
    return (y_prompt, y_sample, new_state_lru)
```

```python
import numpy as np
import concourse.bass as bass
import concourse.mybir as mybir
import concourse.bass_utils as bu
from contextlib import ExitStack

F32 = mybir.dt.float32
BF16 = mybir.dt.bfloat16
AF = mybir.ActivationFunctionType
ALU = mybir.AluOpType

D = 1024
DFF = 2816
NF = 22
LW = 512
NPB = 4
NSB = 8
NQB = 16
EPS = 1e-6


class _Op:
    __slots__ = ("id", "eng", "fn", "cdeps", "ddeps", "sig", "sigval", "dma", "dsem", "dval", "prewait")


class Sched:
    ENGS = ("pe", "act", "dve", "pool", "sp")

    def __init__(self, n_dma_sems=12):
        self.ops = []
        self.by_eng = {e: [] for e in self.ENGS}
        self.lastw = {}
        self.readers = {}
        self.n_dma_sems = n_dma_sems
        self.dma_cnt = {"sp": 0, "pool": 0, "act": 0}
        self.dma_tot = {}
        self.arena_barrier = {}
        self.arena_cur = {}
        self.out_dmas = []

    def arena_switch(self):
        self.arena_barrier = dict(self.arena_cur)
        self.arena_cur = {}

    def op(self, eng, fn, reads=(), writes=(), dma=False, arena=False, out=False):
        o = _Op()
        o.id = len(self.ops)
        o.eng = eng
        o.fn = fn
        o.dma = dma
        o.sig = False
        o.sigval = None
        o.prewait = None
        deps = set()
        for b in reads:
            w = self.lastw.get(b)
            if w is not None:
                deps.add(w)
        for b in writes:
            w = self.lastw.get(b)
            if w is not None:
                deps.add(w)
            deps.update(self.readers.get(b, ()))
        if arena:
            deps.update(self.arena_barrier.values())
        cd = {}
        dd = set()
        for d in deps:
            od = self.ops[d]
            if od.dma:
                dd.add(d)
            else:
                if eng == "pe" and od.eng == "pe":
                    continue
                if od.eng not in cd or cd[od.eng] < d:
                    cd[od.eng] = d
        o.cdeps = cd
        o.ddeps = dd
        if dma:
            k = self.dma_cnt[eng]
            self.dma_cnt[eng] = k + 1
            key = (eng, k % self.n_dma_sems)
            prev = self.dma_tot.get(key, 0)
            o.dsem = key
            o.dval = prev + 16
            o.prewait = prev
            self.dma_tot[key] = prev + 16
            if out:
                self.out_dmas.append(o.id)
        for b in writes:
            self.lastw[b] = o.id
            self.readers[b] = []
        for b in reads:
            self.readers.setdefault(b, []).append(o.id)
        if arena:
            self.arena_cur[eng] = o.id
        self.ops.append(o)
        self.by_eng[eng].append(o)
        return o

    def finalize(self):
        f = _Op()
        f.id = len(self.ops)
        f.eng = "sp"
        f.fn = None
        f.dma = False
        f.sig = False
        f.sigval = None
        f.prewait = None
        f.cdeps = {}
        f.ddeps = set(self.out_dmas)
        self.ops.append(f)
        self.by_eng["sp"].append(f)
        for o in self.ops:
            for d in o.cdeps.values():
                self.ops[d].sig = True
        cnt = {e: 0 for e in self.ENGS}
        for o in self.ops:
            if o.sig:
                cnt[o.eng] += 1
                o.sigval = cnt[o.eng]

    def emit(self, nc, block, esems, dsems):
        ops = self.ops

        def run(engname, e):
            waited = {}

            def w(sem, key, val):
                if val <= 0:
                    return
                if waited.get(key, 0) >= val:
                    return
                waited[key] = val
                e.wait_ge(sem, val)

            for o in self.by_eng[engname]:
                for en, d in o.cdeps.items():
                    w(esems[en], ("e", en), ops[d].sigval)
                for d in o.ddeps:
                    od = ops[d]
                    w(dsems[od.dsem], ("d", od.dsem), od.dval)
                if o.dma:
                    w(dsems[o.dsem], ("d", o.dsem), o.prewait)
                if o.fn is None:
                    continue
                ins = o.fn(e)
                if o.dma:
                    ins.then_inc(dsems[o.dsem], 16)
                elif o.sig:
                    ins.then_inc(esems[engname], 1)

        @block.tensor
        def _(e):
            run("pe", e)

        @block.scalar
        def _(e):
            run("act", e)

        @block.vector
        def _(e):
            run("dve", e)

        @block.gpsimd
        def _(e):
            run("pool", e)

        @block.sync
        def _(e):
            run("sp", e)


def build_program():
    nc = bass.Bass("TRN2", target_bir_lowering=False)
    S = Sched()

    def din(name, shape, dt=F32):
        return nc.dram_tensor(name, list(shape), dt, kind="ExternalInput").ap()

    def dout(name, shape, dt=F32):
        return nc.dram_tensor(name, list(shape), dt, kind="ExternalOutput").ap()

    xp = din("xp", [NPB * 256, D])
    xf = din("xf", [4096, D])
    cvT = din("cvT", [128, 8, 2])
    st0 = din("st0", [128, 2, 4])
    w_mod = din("w_mod", [D, 6 * D])
    bmod_pp = din("bmod_pp", [128, 48])
    bmod_row = din("bmod_row", [1, 6 * D])
    g_pp = din("g_pp", [128, 2, 8])
    gpost_row = din("gpost_row", [2, D])
    w_in = din("w_in", [D, 1536])
    w_pool = din("w_pool", [4, 128, 128])
    psc_pp = din("psc_pp", [128, 4])
    conv5_pp = din("conv5_pp", [128, 4, 5])
    convb_pp = din("convb_pp", [128, 4])
    gwa = din("gwa", [2, 8, 64, 64])
    gwx = din("gwx", [2, 8, 64, 64])
    gba_pp = din("gba_pp", [128, 2, 4])
    gbx_pp = din("gbx_pp", [128, 2, 4])
    lam_pp = din("lam_pp", [128, 2, 4])
    w_out = din("w_out", [D, D])
    w_ffn_in = din("w_ffn_in", [D, 2 * DFF])
    w_ffn_out = din("w_ffn_out", [DFF, D])
    pm_p = din("pm_p", [4, 256, 256])
    pm_s = din("pm_s", [4, 256, 256])
    ident_d = din("ident", [128, 128])
    yp = dout("yp", [NPB * 256, D])
    ys = dout("ys", [NSB * 256, D])
    ns = dout("ns", [32, 128])
    sc_in = nc.dram_tensor("sc_in", [D, 2 * DFF], BF16, kind="Internal").ap()
    sc_out = nc.dram_tensor("sc_out", [DFF, D], BF16, kind="Internal").ap()

    def sb(name, shape, dt=F32):
        return nc.alloc_sbuf_tensor(name, list(shape), dt)

    w_in_sb = sb("w_in_sb", [128, 8, 1536], BF16)
    w_out_sb = sb("w_out_sb", [128, 8, D], BF16)
    gw_sb = sb("gw_sb", [128, 2, 2, 4, 128], BF16)
    pm_one = sb("pm_sb", [128, 4, 2, 256], BF16)
    wpool_sb = sb("wpool_sb", [128, 4, 128], BF16)
    ident = sb("ident_sb", [128, 128], F32)
    GGt = [sb(f"GG{n}", [128, D], F32) for n in range(2)]
    gg_sc = nc.dram_tensor("gg_sc", [2, D], F32, kind="Internal").ap()
    Gp = sb("Gp", [128, 2, 2, 8], F32)
    Sp = sb("Sp", [128, 2, 2, 8], F32)
    cv_sb = sb("cv_sb", [128, 8, 2], F32)
    sc_bf = sb("sc_bf", [128, 8, 2], BF16)
    bmodpp_sb = sb("bmodpp_sb", [128, 48], F32)
    gpp_sb = sb("gpp_sb", [128, 2, 8], F32)
    psc_sb = sb("psc_sb", [128, 4], F32)
    conv5_sb = sb("conv5_sb", [128, 4, 5], F32)
    convb_sb = sb("convb_sb", [128, 4], F32)
    gba_sb = sb("gba_sb", [128, 2, 4], F32)
    gbx_sb = sb("gbx_sb", [128, 2, 4], F32)
    lam_sb = sb("lam_sb", [128, 2, 4], F32)
    hba_sb = sb("hba_sb", [128, 2, 4], F32)
    hbx_sb = sb("hbx_sb", [128, 2, 4], F32)
    cL_sb = sb("cL_sb", [128, 2, 4], F32)
    cLh_sb = sb("cLh_sb", [128, 2, 4], F32)
    st_sb = sb("st_sb", [128, 2, 4], F32)
    st0_sb = sb("st0_sb", [128, 2, 4], F32)
    mhalf = sb("mhalf", [128, 8], F32)
    nsb = sb("nsb", [128, 32], F32)
    nsT = sb("nsT", [32, 128], F32)
    ss = sb("ss", [128, 16], F32)
    var = sb("var", [128, 16], F32)
    rstd = sb("rstd", [128, 16], F32)
    uhead = sb("uhead", [128, 4, 9, 2], F32)
    utail = sb("utail", [128, 4, 2], F32)
    HQ = sb("HQ", [128, 4, 2048], BF16)
    X1 = sb("X1", [128, 4, 1024], F32)
    xs = [sb(f"xs{i}", [128, D], F32) for i in range(2)]
    tmpr = [sb(f"tmpr{i}", [128, D], BF16) for i in range(2)]
    hT = sb("hT", [128, 8, 512], BF16)
    wfi = [sb(f"wfi{i}", [128, 8, 2, 128], BF16) for i in range(2)]
    wfo = [sb(f"wfo{i}", [128, NF, 128], BF16) for i in range(2)]
    xq = [sb(f"xq{i}", [128, D], F32) for i in range(2)]
    xin = [xq[0], xq[1], sb("xin2", [128, D], F32), sb("xin3", [128, D], F32)]
    xin_n = ["xq0", "xq1", "xin2", "xin3"]

    ARENA_BYTES = 42 * 1024
    arena = sb("arena", [128, ARENA_BYTES // 4], F32)
    _off = [0]

    def carve(shape, dt, reset=False):
        if reset:
            _off[0] = 0
        n = int(np.prod(shape))
        nb = n * (2 if dt == BF16 else 4)
        nb_al = (nb + 63) // 64 * 64
        o = _off[0]
        assert o + nb_al <= ARENA_BYTES, (o, nb_al)
        _off[0] = o + nb_al
        a = arena[:, o // 4:(o + nb) // 4]
        if dt == BF16:
            a = a.bitcast(BF16)
        if len(shape) == 2:
            return a.rearrange("p (a b) -> p a b", a=shape[0], b=shape[1])
        if len(shape) == 3:
            return a.rearrange("p (a b c) -> p a b c", a=shape[0], b=shape[1], c=shape[2])
        return a

    upool = sb("upool", [128, 2, 512], BF16)
    mT = sb("mT", [128, 4, 256], BF16)
    ypT2 = [sb("ypT0", [128, 4, 256], BF16), None]
    gel2 = [sb("gel0", [128, 4, 256], BF16), None]
    ubuf_m = [sb("ubm0", [128, 4, 260], F32), None]
    xba = sb("xba", [128, 4, 256], F32)
    xbba = sb("xbba", [128, 4, 256], BF16)
    NCH = 4
    TT = carve([4, 2, 256], F32, reset=True)
    T3a = carve([4, 256], F32)
    hOa = carve([4, 256], F32)
    T1v = TT[:, :, 0, :]
    T2v = TT[:, :, 1, :]
    lru_end = _off[0]
    ypT2[1] = carve([4, 256], BF16)
    gel2[1] = carve([4, 256], BF16)
    ylT = carve([4, 256], BF16)
    ubuf_m[1] = carve([4, 260], F32)
    _off[0] = lru_end
    NUQ = 5
    ubuf_q = [carve([4, 260], F32) for _ in range(NUQ)]
    actT = carve([NF, 512], BF16, reset=True)
    ffT = carve([8, 512], F32)
    sg = [carve([512], F32) for _ in range(2)]

    PS = [nc.alloc_psum_tensor(f"ps{i}", [128, 1024], F32) for i in range(4)]
    bank_ctr = [0]

    def _bank_free(b):
        nm = f"bank{b}"
        assert S.lastw.get(nm) is None or len(S.readers.get(nm, ())) > 0, f"PSUM bank {b} reused before evacuation"

    def bank():
        b = bank_ctr[0] % 8
        bank_ctr[0] += 1
        _bank_free(b)
        return b

    def bank2():
        if bank_ctr[0] % 2:
            bank_ctr[0] += 1
        b = bank_ctr[0] % 8
        bank_ctr[0] += 2
        _bank_free(b)
        _bank_free(b + 1)
        return b

    def bap(b, w=512):
        return PS[b // 2][:, (b % 2) * 512:(b % 2) * 512 + w]

    def bn(b):
        return f"bank{b}"

    def mm(out, lhsT, rhs, start, stop, reads, writes, arena=False):
        S.op("pe", lambda e: e.matmul(out, lhsT, rhs, start=start, stop=stop), reads, writes, arena=arena)

    def tr(out, in_, reads, writes, arena=False):
        S.op("pe", lambda e: e.transpose(out, in_, ident[:]), list(reads) + ["ident"], writes, arena=arena)

    def act(out, in_, func, reads, writes, bias=None, scale=None, accum=None, arena=False):
        kw = {}
        if bias is not None:
            kw["bias"] = bias
        if scale is not None:
            kw["scale"] = scale
        if accum is not None:
            kw["accum_out"] = accum
        S.op("act", lambda e: e.activation(out, in_, func, **kw), reads, writes, arena=arena)

    def ts(out, in0, s1, s2, op0, op1, reads, writes, eng="dve", arena=False):
        if op1 is None:
            S.op(eng, lambda e: e.tensor_scalar(out, in0, s1, None, op0), reads, writes, arena=arena)
        else:
            S.op(eng, lambda e: e.tensor_scalar(out, in0, s1, s2, op0, op1), reads, writes, arena=arena)

    def tt(out, in0, in1, op, reads, writes, eng="dve", arena=False):
        S.op(eng, lambda e: e.tensor_tensor(out, in0, in1, op), reads, writes, arena=arena)

    def stt(out, in0, sc, in1, op0, op1, reads, writes, arena=False):
        S.op("dve", lambda e: e.scalar_tensor_tensor(out, in0, sc, in1, op0, op1), reads, writes, arena=arena)

    def cp(out, in_, reads, writes, eng="dve", arena=False):
        if eng == "act":
            S.op("act", lambda e: e.activation(out, in_, AF.Copy), reads, writes, arena=arena)
        else:
            S.op(eng, lambda e: e.tensor_copy(out, in_), reads, writes, arena=arena)

    def mset(ap, val, writes, eng="pool", arena=False):
        S.op(eng, lambda e: e.memset(ap, val), (), writes, arena=arena)

    def dma(q, out, in_, reads, writes, arena=False, is_out=False):
        S.op(q, lambda e: e.dma_start(out=out, in_=in_), reads, writes, dma=True, arena=arena, out=is_out)

    for dst, src, nm in [(ident[:], ident_d, "ident"), (cv_sb[:], cvT, "cv"), (bmodpp_sb[:], bmod_pp, "bmodpp"),
                         (gpp_sb[:], g_pp, "gpp"), (psc_sb[:], psc_pp, "psc"), (conv5_sb[:], conv5_pp, "conv5"),
                         (convb_sb[:], convb_pp, "convb"), (gba_sb[:], gba_pp, "gba"), (gbx_sb[:], gbx_pp, "gbx"),
                         (lam_sb[:], lam_pp, "lam"), (st0_sb[:], st0, "st0")]:
        dma("sp", dst, src, (), [nm])
    mset(mhalf[:], -0.5, ["mhalf"])
    mset(gw_sb[:], 0.0, ["gw"])
    mset(nsb[:], 0.0, ["nsb"])

    act(sc_bf[:], cv_sb[:], AF.Silu, ["cv"], ["scbf"])
    def screp(v, kc):
        return hT[:, kc, 256 + v * 128:256 + (v + 1) * 128]
    for v in range(2):
        for kc in range(8):
            cp(screp(v, kc), sc_bf[:, kc, v:v + 1].to_broadcast([128, 128]), ["scbf"], ["hT1"])

    ts(hba_sb[:], gba_sb[:], 0.5, None, ALU.mult, None, ["gba"], ["hba"])
    ts(hbx_sb[:], gbx_sb[:], 0.5, None, ALU.mult, None, ["gbx"], ["hbx"])
    act(cL_sb[:], lam_sb[:], AF.Exp, ["lam"], ["cL"], scale=-1.0)
    ts(cL_sb[:], cL_sb[:], 1.0, None, ALU.add, None, ["cL"], ["cL"])
    act(cLh_sb[:], cL_sb[:], AF.Ln, ["cL"], ["cLh"])
    ts(cL_sb[:], cLh_sb[:], -8.0, None, ALU.mult, None, ["cLh"], ["cL"])
    ts(cLh_sb[:], cLh_sb[:], -4.0, None, ALU.mult, None, ["cLh", "cL"], ["cLh"])

    gstage = xin[3][:].rearrange("p (a c j) -> p a c j", a=4, c=4, j=64)
    for d in range(2):
        for ax, src in enumerate((gwa, gwx)):
            dma("sp", gstage[:, d * 2 + ax, :, :], src[d].rearrange("(c hh) i j -> (hh i) c j", hh=2), (), [f"gst{d}{ax}"])
    for d in range(2):
        for ax in range(2):
            for hh in range(2):
                cp(gw_sb[hh * 64:(hh + 1) * 64, d, ax, :, hh * 64:(hh + 1) * 64],
                   gstage[hh * 64:(hh + 1) * 64, d * 2 + ax, :, :], [f"gst{d}{ax}", "gw", "xin3"], [f"gwd{d}{ax}"])

    wmod_v = w_mod.rearrange("(kc p) n -> p kc n", p=128)
    X1_names = [f"X1_{q}" for q in range(4)]
    stage_X1 = X1[:].bitcast(BF16).rearrange("p a (b c) -> p (a b) c", b=2, c=1024)
    stage_HQ = HQ[:].rearrange("p a (b c) -> p (a b) c", b=2, c=1024)

    def mod_section(sec, wbuf, wnames):
        dma("pool", wbuf, wmod_v[:, :, sec * D:(sec + 1) * D], (), wnames)
        if sec in (0, 1, 3, 4):
            b = bank()
            pv = bap(b, 16).rearrange("p (c v) -> p c v", c=8, v=2)
            for ch in range(8):
                for kc in range(8):
                    mm(pv[:, ch, :], wbuf[:, kc, ch * 128:(ch + 1) * 128], sc_bf[:, kc, :], kc == 0, kc == 7,
                       list(wnames) + ["scbf"], [bn(b)])
            nrm = 0 if sec < 3 else 1
            for v in range(2):
                if sec in (0, 3):
                    tt(Sp[:, nrm, v, :], pv[:, :, v], bmodpp_sb[:, sec * 8:(sec + 1) * 8], ALU.add,
                       [bn(b), "bmodpp"], ["Sp"])
                else:
                    tt(Gp[:, nrm, v, :], pv[:, :, v], bmodpp_sb[:, sec * 8:(sec + 1) * 8], ALU.add,
                       [bn(b), "bmodpp"], ["Gp"])
                    stt(Gp[:, nrm, v, :], Gp[:, nrm, v, :], 1.0, gpp_sb[:, nrm, :], ALU.add, ALU.mult,
                        ["Gp", "gpp"], ["Gp"])
        else:
            nrm = 0 if sec == 2 else 1
            gprow = xin[2][:]
            dma("sp", gprow, gpost_row[nrm:nrm + 1, :].partition_broadcast(128), (), ["xin2"])
            for v in (1, 0):
                dma("sp", GGt[nrm][:], bmod_row[0:1, sec * D:(sec + 1) * D].partition_broadcast(128), (), [f"GG{nrm}"])
                for half in range(2):
                    b = bank()
                    for kc in range(8):
                        mm(bap(b), screp(v, kc), wbuf[:, kc, half * 512:(half + 1) * 512], kc == 0, kc == 7,
                           list(wnames) + ["hT1"], [bn(b)])
                    gsl = GGt[nrm][:, half * 512:(half + 1) * 512]
                    tt(gsl, bap(b), gsl, ALU.add, [bn(b), f"GG{nrm}"], [f"GG{nrm}"])
                    tt(gsl, gsl, gprow[:, half * 512:(half + 1) * 512], ALU.mult, [f"GG{nrm}", "xin2"], [f"GG{nrm}"])
                if v == 1:
                    dma("sp", gg_sc[nrm:nrm + 1, :], GGt[nrm][0:1, :], [f"GG{nrm}"], [f"ggsc{nrm}"])

    mod_section(0, stage_X1, X1_names)
    mod_section(1, stage_HQ, ["HQ"])
    dma("pool", w_in_sb[:], w_in.rearrange("(kc p) n -> p kc n", p=128), (), ["w_in"])

    deferred = []
    for sec in (2, 3, 4, 5):
        deferred.append(lambda sec=sec: mod_section(sec, stage_X1, X1_names))
    deferred.append(lambda: dma("pool", pm_one[:], pm_p.rearrange("g (sc p) t -> p g sc t", p=128), (), ["pm"]))
    deferred.append(lambda: (dma("pool", wpool_sb[:], w_pool.rearrange("g c d -> c g d"), (), ["wpool"]),
                             dma("pool", w_out_sb[:], w_out.rearrange("(kc p) n -> p kc n", p=128), (), ["w_out"])))
    for i in range(4):
        deferred.append(lambda i=i: dma("pool", sc_in[:, i * 1408:(i + 1) * 1408],
                                        w_ffn_in[:, i * 1408:(i + 1) * 1408], (), [f"sc_in{i}"]))
    for i in range(2):
        deferred.append(lambda i=i: dma("pool", sc_out[i * 1408:(i + 1) * 1408, :],
                                        w_ffn_out[i * 1408:(i + 1) * 1408, :], (), [f"sc_out{i}"]))

    S.arena_switch()

    ss_ctr = [0]

    def norm_stats(src_tiles, src_names):
        cols = []
        for t in range(2):
            c = ss_ctr[0] % 16
            ss_ctr[0] += 1
            cols.append(c)
            act(xs[t % 2][:], src_tiles[t], AF.Square, [src_names[t]], [f"xs{t % 2}", f"ss{c}"], accum=ss[:, c:c + 1])
            ts(var[:, c:c + 1], ss[:, c:c + 1], 1.0 / D, EPS, ALU.mult, ALU.add, [f"ss{c}"], [f"var{c}"], eng="pool")
            tt(rstd[:, c:c + 1], var[:, c:c + 1], mhalf[:, 0:1], ALU.pow, [f"var{c}", "mhalf"], [f"rstd{c}"], eng="pool")
        return cols

    def norm_transpose(src_tiles, src_names, nrm, v, ntile, hsel=0, evac="act"):
        cols = norm_stats(src_tiles, src_names)
        norm_apply(src_tiles, src_names, cols, nrm, v, hsel, evac)

    def norm_scale(src_tiles, src_names, cols):
        for t in range(2):
            c = cols[t]
            act(xs[t % 2][:], src_tiles[t], AF.Copy, [src_names[t], f"rstd{c}"], [f"xs{t % 2}"], scale=rstd[:, c:c + 1])

    def norm_tr():
        bks = [bank() for _ in range(4)]
        for t in range(2):
            xsb = xs[t % 2]
            for kp in range(4):
                for k2 in range(2):
                    kc = kp * 2 + k2
                    o = (k2 * 2 + t) * 128
                    tr(bap(bks[kp])[:, o:o + 128], xsb[:, kc * 128:(kc + 1) * 128], [f"xs{t % 2}"], [bn(bks[kp])])
        return bks

    def norm_scale_tr(src_tiles, src_names, cols):
        bks = [bank() for _ in range(4)]
        for t in range(2):
            c = cols[t]
            xsb = xs[t % 2]
            act(xsb[:], src_tiles[t], AF.Copy, [src_names[t], f"rstd{c}"], [f"xs{t % 2}"], scale=rstd[:, c:c + 1])
            for kp in range(4):
                for k2 in range(2):
                    kc = kp * 2 + k2
                    o = (k2 * 2 + t) * 128
                    tr(bap(bks[kp])[:, o:o + 128], xsb[:, kc * 128:(kc + 1) * 128], [f"xs{t % 2}"], [bn(bks[kp])])
        return bks

    def norm_evac(bks, nrm, v, hsel=0, evac="act"):
        base = hsel * 256
        for kp in range(4):
            for k2 in range(2):
                kc = kp * 2 + k2
                if evac == "act":
                    act(hT[:, kc, base:base + 256], bap(bks[kp])[:, k2 * 256:(k2 + 1) * 256], AF.Identity,
                        [bn(bks[kp]), "Gp", "Sp"], [f"hT{hsel}"], bias=Sp[:, nrm, v, kc:kc + 1], scale=Gp[:, nrm, v, kc:kc + 1])
                else:
                    ts(hT[:, kc, base:base + 256], bap(bks[kp])[:, k2 * 256:(k2 + 1) * 256],
                       Gp[:, nrm, v, kc:kc + 1], Sp[:, nrm, v, kc:kc + 1], ALU.mult, ALU.add,
                       [bn(bks[kp]), "Gp", "Sp"], [f"hT{hsel}"])

    def norm_apply(src_tiles, src_names, cols, nrm, v, hsel=0, evac="act"):
        norm_evac(norm_scale_tr(src_tiles, src_names, cols), nrm, v, hsel, evac)

    def xpart_mm(hsel=0):
        hb = hsel * 256
        bks = []
        for cp_ in range(2):
            b = bank()
            bks.append(b)
            for c2 in range(2):
                cc = cp_ * 2 + c2
                for kc in range(8):
                    mm(bap(b)[:, c2 * 256:(c2 + 1) * 256], w_in_sb[:, kc, 512 + cc * 128:512 + (cc + 1) * 128],
                       hT[:, kc, hb:hb + 256], kc == 0, kc == 7, ["w_in", f"hT{hsel}"], [bn(b)])
        return bks

    def xpart_evac(bks, ub, ubname, ar=True):
        for cp_, b in enumerate(bks):
            cp(ub[:, cp_ * 2:cp_ * 2 + 2, 2:258], bap(b).rearrange("p (c n) -> p c n", c=2), [bn(b)], [ubname],
               eng="act", arena=ar)

    def xpart(n, ub, ubname, hsel=0, ar=True):
        xpart_evac(xpart_mm(hsel), ub, ubname, ar)

    XBN = [f"xb{cc}" for cc in range(4)]
    XBSET = [(xba[:], xbba[:], XBN, "xbb"),
             (ubuf_m[0][:, :, 0:256], ypT2[0][:], ["ubm0"], "ypT0")]

    def conv_all(ub, ubname, ar=True, xset=0):
        xf32, xb16, xn, xbn = XBSET[xset]
        for cc in range(4):
            nm = [xn[cc]] if len(xn) == 4 else xn
            ts(xf32[:, cc, :], ub[:, cc, 0:256], conv5_sb[:, cc, 0:1], convb_sb[:, cc:cc + 1], ALU.mult, ALU.add,
               [ubname, "conv5", "convb"], nm, arena=ar)
            for k in range(1, 5):
                stt(xf32[:, cc, :], ub[:, cc, k:k + 256], conv5_sb[:, cc, k:k + 1], xf32[:, cc, :], ALU.mult, ALU.add,
                    [ubname, "conv5"] + nm, nm, arena=ar)
        cp(xb16, xf32, xn, [xbn], arena=ar)

    ALLC = list(range(4))

    T1n = [f"T1{i}" for i in ALLC]
    T2n = [f"T2{i}" for i in ALLC]
    T3n_ = [f"T3{i}" for i in ALLC]
    hOn_ = [f"hO{i}" for i in ALLC]

    def wave_gates(chains, xset=0):
        xb16, xbn = XBSET[xset][1], XBSET[xset][3]
        banks = []
        for i, c in enumerate(chains):
            b = bank()
            banks.append(b)
            cc, d = c["cc"], c["d"]
            mm(bap(b)[:, 0:256], gw_sb[:, d, 0, cc, :], xb16[:, cc, :], True, True,
               ["gw", f"gwd{d}0", xbn], [bn(b)], arena=True)
            mm(bap(b)[:, 256:512], gw_sb[:, d, 1, cc, :], xb16[:, cc, :], True, True,
               ["gw", f"gwd{d}1", xbn], [bn(b)], arena=True)
        return banks

    def wave_act1(chains, banks, xsel):
        for i, c in enumerate(chains):
            cc, d = c["cc"], c["d"]
            b = banks[i]
            act(TT[:, i, 0, :], bap(b)[:, 0:256], AF.Tanh, [bn(b), "hba"], [f"T1{i}"], bias=hba_sb[:, d, cc:cc + 1], scale=0.5, arena=True)
            act(TT[:, i, 1, :], bap(b)[:, 256:512], AF.Tanh, [bn(b), "hbx"], [f"T2{i}"], bias=hbx_sb[:, d, cc:cc + 1], scale=0.5, arena=True)
        for i, c in enumerate(chains):
            cc, d = c["cc"], c["d"]
            act(T3a[:, i, :], TT[:, i, 0, :], AF.Exp, [f"T1{i}", "cLh"], [f"T3{i}"], bias=cLh_sb[:, d, cc:cc + 1],
                scale=cLh_sb[:, d, cc:cc + 1], arena=True)
        act(T1v, T3a[:], AF.Square, T3n_, T1n, arena=True)
        ts(T1v, T1v, 1.0, -1.0, ALU.min, ALU.mult, T1n, T1n, arena=True)
        for (t2view, xbview, names) in xsel:
            stt(t2view, t2view, 1.0, xbview, ALU.add, ALU.mult, T2n + names, T2n, arena=True)

    def wave_fin(chains):
        act(T1v, T1v, AF.Sqrt, T1n, T1n, bias=1.0, scale=1.0, arena=True)
        stt(T2v, T2v, 0.5, T1v, ALU.mult, ALU.mult, T2n + T1n, T2n, arena=True)
        for i, c in enumerate(chains):
            init_ap, init_names = c["init"], c["init_names"]
            if c["rev"]:
                S.op("dve", lambda e, i=i, init_ap=init_ap: e.tensor_tensor_scan(
                    hOa[:, i, ::-1], T3a[:, i, ::-1], TT[:, i, 1, ::-1], init_ap, ALU.mult, ALU.add),
                     [f"T3{i}", f"T2{i}"] + list(init_names), [f"hO{i}"], arena=True)
            else:
                S.op("dve", lambda e, i=i, init_ap=init_ap: e.tensor_tensor_scan(
                    hOa[:, i, :], T3a[:, i, :], TT[:, i, 1, :], init_ap, ALU.mult, ALU.add),
                     [f"T3{i}", f"T2{i}"] + list(init_names), [f"hO{i}"], arena=True)
        return T3n_, hOn_

    def lru_wave(chains, xsel):
        banks = wave_gates(chains)
        wave_act1(chains, banks, xsel)
        return wave_fin(chains)

    def q_load_stats(j):
        tiles = []
        names = []
        for t in range(2):
            i = (2 * j + t) % 2
            dma("sp", xq[i][:], xf[j * 256 + t * 128:j * 256 + (t + 1) * 128, :], (), [f"xq{i}"])
            tiles.append(xq[i][:])
            names.append(f"xq{i}")
        return tiles, names, norm_stats(tiles, names)

    def q_front_2(j, bks):
        ub = ubuf_q[j % NUQ]
        ubn = f"ubq{j % NUQ}"
        xpart_evac(bks, ub, ubn)
        if j <= 8:
            cp(uhead[:, :, j, :], ub[:, :, 2:4], [ubn], ["uhead"], eng="pool", arena=True)
        if j == NQB - 1:
            mset(ub[:, :, 258:260], 0.0, [ubn], eng="pool", arena=True)
        else:
            up = ubuf_q[(j + 1) % NUQ]
            upn = f"ubq{(j + 1) % NUQ}"
            cp(ub[:, :, 258:260], up[:, :, 2:4], [upn], [ubn], eng="pool", arena=True)
            cp(up[:, :, 0:2], ub[:, :, 256:258], [ubn], [upn], eng="pool", arena=True)
        if j == 0:
            mset(ub[:, :, 0:2], 0.0, [ubn], eng="pool", arena=True)

    def q_chains():
        return [dict(cc=cc, d=1, rev=True, init=st_sb[:, 1, cc:cc + 1], init_names=["st"]) for cc in range(4)]

    def q_back_fin(j, chains):
        T3n, hOn = wave_fin(chains)
        cp(st_sb[:, 1, :], hOa[:, :, 0], hOn, ["st"], arena=True)
        if j < NSB:
            cp(HQ[:, :, j * 256:(j + 1) * 256], hOa[:], hOn, ["HQ"], arena=True)

    q_pend = []
    q_gates = []

    def q_iter(jf, jc, jw):
        ok = lambda x: x is not None and 0 <= x < NQB
        chains = q_chains() if ok(jw) else None
        gbanks = q_gates.pop() if ok(jw) else None
        if ok(jc):
            conv_all(ubuf_q[jc % NUQ], f"ubq{jc % NUQ}", xset=jc % 2)
        if ok(jw):
            xs_ = XBSET[jw % 2]
            wave_act1(chains, gbanks, [(T2v, xs_[0], xs_[2])])
        if q_pend:
            q_front_2(*q_pend.pop())
        if deferred and (jf is None or jf <= NQB - 2):
            deferred.pop(0)()
        if ok(jf):
            tiles, names, cols = q_load_stats(jf)
            tbanks = norm_scale_tr(tiles, names, cols)
        if ok(jw):
            q_back_fin(jw, chains)
        if ok(jf):
            norm_evac(tbanks, 0, 1, 0)
        if ok(jc):
            q_gates.append(wave_gates(q_chains(), xset=jc % 2))
        if ok(jf):
            q_pend.append((jf, xpart_mm(0)))

    pre_state = {}

    def pre_a(src, v, hs):
        tiles = []
        names = []
        for t in range(2):
            i = 2 * hs + t
            dma("sp", xin[i][:], src[t * 128:(t + 1) * 128, :], (), [xin_n[i]])
            tiles.append(xin[i][:])
            names.append(xin_n[i])
        pre_state[hs] = (tiles, names, norm_stats(tiles, names), v)

    def pre_b(hs):
        tiles, names, cols, v = pre_state.pop(hs)
        norm_apply(tiles, names, cols, 0, v, hs)

    def pre_b1(hs):
        tiles, names, cols, v = pre_state[hs]
        norm_scale(tiles, names, cols)

    def pre_b2(hs):
        tiles, names, cols, v = pre_state.pop(hs)
        norm_evac(norm_tr(), 0, v, hs)

    def mixer_front(kind, j, v, hs):
        ypT, gel = ypT2[hs], gel2[hs]
        hb = hs * 256
        ar = (hs == 1)
        for t in range(2):
            b = bank()
            for kc in range(8):
                mm(bap(b), hT[:, kc, hb + t * 128:hb + (t + 1) * 128], w_in_sb[:, kc, 0:512], kc == 0, kc == 7,
                   [f"hT{hs}", "w_in"], [bn(b)])
            cp(upool[:, t, :], bap(b), [bn(b)], ["upool"], eng="act")
        for cc in range(4):
            b = bank()
            for kc in range(8):
                mm(bap(b, 256), w_in_sb[:, kc, 1024 + cc * 128:1024 + (cc + 1) * 128], hT[:, kc, hb:hb + 256], kc == 0, kc == 7,
                   ["w_in", f"hT{hs}"], [bn(b)])
            act(gel[:, cc, :], bap(b, 256), AF.Gelu_apprx_tanh, [bn(b)], [f"gel{hs}"], arena=ar)
        ub = ubuf_m[hs]
        ubn = f"ubm{hs}"
        xpart(256, ub, ubn, hsel=hs, ar=ar)
        for g in range(4):
            b = bank()
            for t in range(2):
                mm(bap(b, 256), upool[:, t, g * 128:(g + 1) * 128], pm_one[:, g, t, :], t == 0, t == 1,
                   ["upool", "pm"], [bn(b)])
            cp(mT[:, g, :], bap(b, 256), [bn(b)], ["mT"])
        for g in range(4):
            b = bank()
            mm(bap(b, 256), wpool_sb[:, g, :], mT[:, g, :], True, True, ["wpool", "mT"], [bn(b)])
            ts(ypT[:, g, :], bap(b, 256), psc_sb[:, g:g + 1], None, ALU.mult, None, [bn(b), "psc"], [f"ypT{hs}"], arena=ar)
        if kind == 0 or j == 0:
            mset(ub[:, :, 0:2], 0.0, [ubn], eng="pool", arena=ar)
        else:
            cp(ub[:, :, 0:2], utail[:], ["utail"], [ubn], eng="pool", arena=ar)
        if kind == 0:
            mset(ub[:, :, 258:260], 0.0, [ubn], eng="pool", arena=ar)
        else:
            cp(ub[:, :, 258:260], uhead[:, :, j + 1, :], ["uhead"], [ubn], eng="pool", arena=ar)
            cp(utail[:], ub[:, :, 256:258], [ubn], ["utail"], eng="pool", arena=ar)

    def mixer_tail_a(hs):
        conv_all(ubuf_m[hs], f"ubm{hs}", ar=(hs == 1))

    def mixer_tail(kind, j, slot0, v, hs):
        ypT, gel = ypT2[hs], gel2[hs]
        if kind == 0:
            col = j * 8
            for half in range(2):
                cA, cB = 2 * half, 2 * half + 1
                chains = [dict(cc=cA, d=0, rev=False, init=0.0, init_names=[]),
                          dict(cc=cA, d=1, rev=True, init=0.0, init_names=[]),
                          dict(cc=cB, d=0, rev=False, init=0.0, init_names=[]),
                          dict(cc=cB, d=1, rev=True, init=0.0, init_names=[])]
                xsel = [(TT[:, dd::2, 1, :], xba[:, cA:cB + 1, :], XBN) for dd in range(2)]
                T3n, hOn = lru_wave(chains, xsel)
                cp(nsb[:, col + cA:col + cA + 2], hOa[:, 0::2, 255], hOn, ["nsb"], arena=True)
                cp(nsb[:, col + 4 + cA:col + 4 + cA + 2], hOa[:, 1::2, 0], hOn, ["nsb"], arena=True)
                tt(T3a[:, 0:2, :], hOa[:, 0::2, :], hOa[:, 1::2, :], ALU.add, hOn, T3n, arena=True)
                tt(ylT[:, cA:cB + 1, :], T3a[:, 0:2, :], gel[:, cA:cB + 1, :], ALU.mult, T3n + [f"gel{hs}"], ["ylT"], arena=True)
        else:
            chains = [dict(cc=cc, d=0, rev=False, init=st_sb[:, 0, cc:cc + 1], init_names=["st"]) for cc in range(4)]
            T3n, hOn = lru_wave(chains, [(T2v, xba[:], XBN)])
            cp(st_sb[:, 0, :], hOa[:, :, 255], hOn, ["st"], arena=True)
            tt(T3a[:], hOa[:], HQ[:, :, j * 256:(j + 1) * 256], ALU.add, hOn + ["HQ"], T3n, arena=True)
            tt(ylT[:], T3a[:], gel[:], ALU.mult, T3n + [f"gel{hs}"], ["ylT"], arena=True)
    def tail_out_mm(slot0, hs):
        ypT = ypT2[hs]
        pend = []
        for t in range(2):
            s_ = slot0 + t
            b = bank2()
            for half in range(2):
                for kc in range(8):
                    lhs = ypT[:, kc, t * 128:(t + 1) * 128] if kc < 4 else ylT[:, kc - 4, t * 128:(t + 1) * 128]
                    mm(bap(b + half), lhs, w_out_sb[:, kc, half * 512:(half + 1) * 512], kc == 0, kc == 7,
                       [f"ypT{hs}", "ylT", "w_out"], [bn(b + half)], arena=True)
            pend.append((b, s_, post_norm_stats(b, s_)))
        return pend

    def tail_out_apply(pend, v, hs):
        for t, (b, s_, c) in enumerate(pend):
            post_norm_apply(b, s_, c, 0, v, res=xin[2 * hs + t][:], res_name=xin_n[2 * hs + t])

    def post_norm_stats(b, s):
        full = PS[b // 2][:, :]
        c = ss_ctr[0] % 16
        ss_ctr[0] += 1
        act(tmpr[s % 2][:], full, AF.Square, [bn(b), bn(b + 1)], [f"tmpr{s % 2}", f"ss{c}"], accum=ss[:, c:c + 1])
        ts(var[:, c:c + 1], ss[:, c:c + 1], 1.0 / D, EPS, ALU.mult, ALU.add, [f"ss{c}"], [f"var{c}"], eng="pool")
        tt(rstd[:, c:c + 1], var[:, c:c + 1], mhalf[:, 0:1], ALU.pow, [f"var{c}", "mhalf"], [f"rstd{c}"], eng="pool")
        return c

    def post_norm_apply(b, s, c, nrm, v, res=None, res_name=None):
        full = PS[b // 2][:, :]
        stt(full, full, rstd[:, c:c + 1], GGt[nrm][:], ALU.mult, ALU.mult,
            [bn(b), bn(b + 1), f"rstd{c}", f"GG{nrm}"], [bn(b), bn(b + 1)])
        if res is None:
            res, res_name = X1[:, s, :], f"X1_{s}"
        tt(X1[:, s, :], full, res, ALU.add, [bn(b), bn(b + 1), res_name, f"X1_{s}"], [f"X1_{s}"])

    wf_ctr = [0, 0]
    sc_in_v = sc_in.rearrange("(kc p) n -> p kc n", p=128)
    sc_out_v = sc_out.rearrange("(f p) n -> p f n", p=128)

    def ffn_norm(v, hs):
        tiles = [X1[:, s, :] for s in range(2 * hs, 2 * hs + 2)]
        names = [f"X1_{s}" for s in range(2 * hs, 2 * hs + 2)]
        norm_transpose(tiles, names, 1, v, 2, hsel=hs, evac="dve")

    def ffn_pair(v, dst, hook=None, mid=None, hook2=None):
        S.arena_switch()
        for f in range(NF):
            if mid is not None:
                mid(f)
            i = wf_ctr[0] % 2
            wf_ctr[0] += 1
            for gu in range(2):
                dma("sp", wfi[i][:, :, gu, :], sc_in_v[:, :, gu * DFF + f * 128:gu * DFF + (f + 1) * 128],
                    [f"sc_in{q}" for q in range(4)], [f"wfi{i}{gu}"])
            bg = bank()
            for kc in range(8):
                mm(bap(bg), wfi[i][:, kc, 0, :], hT[:, kc, :], kc == 0, kc == 7, [f"wfi{i}0", "hT0", "hT1"], [bn(bg)])
            bu_ = bank()
            for kc in range(8):
                mm(bap(bu_), wfi[i][:, kc, 1, :], hT[:, kc, :], kc == 0, kc == 7, [f"wfi{i}1", "hT0", "hT1"], [bn(bu_)])
            act(sg[f % 2], bap(bg), AF.Silu, [bn(bg)], [f"sg{f % 2}"], arena=True)
            tt(actT[:, f, :], sg[f % 2], bap(bu_), ALU.mult, [f"sg{f % 2}", bn(bu_)], ["actT"], arena=True)
        if hook is not None:
            hook()
        for m in range(8):
            i = wf_ctr[1] % 2
            wf_ctr[1] += 1
            dma("sp", wfo[i][:], sc_out_v[:, :, m * 128:(m + 1) * 128], ["sc_out0", "sc_out1"], [f"wfo{i}"])
            b = bank()
            for f in range(NF):
                mm(bap(b), wfo[i][:, f, :], actT[:, f, :], f == 0, f == NF - 1, [f"wfo{i}", "actT"], [bn(b)], arena=True)
            cp(ffT[:, m, :], bap(b), [bn(b)], ["ffT"], eng="act", arena=True)
            if hook2 is not None:
                hook2(m)
        pend = []
        for t in range(4):
            b = bank2()
            for m in range(8):
                tr(PS[b // 2][:, m * 128:(m + 1) * 128], ffT[:, m, t * 128:(t + 1) * 128], ["ffT"], [bn(b + m // 4)],
                   arena=True)
            pend.append((b, t, post_norm_stats(b, t)))
        for (b, t, c) in pend:
            post_norm_apply(b, t, c, 1, v)
            dma("pool", dst[t * 128:(t + 1) * 128, :], X1[:, t, :], [f"X1_{t}"], [f"ydst{len(S.ops)}"], is_out=True)
        S.arena_switch()

    cp(st_sb[:], st0_sb[:], ["st0"], ["st"])
    for j in range(NQB - 1, -5, -1):
        q_iter(j, j + 3, j + 4)
    while deferred:
        deferred.pop(0)()
    S.arena_switch()
    blocks = [(xp[q * 256:(q + 1) * 256, :], 0, q, 0) for q in range(4)] + \
             [(xf[j * 256:(j + 1) * 256, :], 1, j, 1) for j in range(NSB)]
    dsts = [yp[0:512, :], yp[512:1024, :]] + [ys[q * 512:(q + 1) * 512, :] for q in range(4)]

    def prea(i):
        src, kind, j, v = blocks[i]
        pre_a(src, v, i % 2)

    def front_A(p):
        _, kind, j0, v = blocks[2 * p]
        mixer_front(kind, j0, v, 0)
        mixer_tail_a(0)

    prea(0)
    pre_b(0)
    prea(1)
    pre_b(1)
    front_A(0)
    for p in range(6):
        i0, i1 = 2 * p, 2 * p + 1
        _, kind, j0, v = blocks[i0]
        j1 = blocks[i1][2]
        mixer_front(kind, j1, v, 1)
        if p == 1:
            dma("pool", pm_one[:], pm_s.rearrange("g (sc p) t -> p g sc t", p=128), ["pm"], ["pm"])
        mixer_tail(kind, j0, 0, v, 0)
        pend0 = tail_out_mm(0, 0)
        mixer_tail_a(1)
        tail_out_apply(pend0, v, 0)
        mixer_tail(kind, j1, 2, v, 1)
        ffn_norm(v, 0)
        pend1 = tail_out_mm(2, 1)
        tail_out_apply(pend1, v, 1)
        ffn_norm(v, 1)

        def mid(f, p=p):
            if p < 5:
                if f == 3:
                    prea(2 * p + 2)
                elif f == 8:
                    prea(2 * p + 3)

        def hook(p=p):
            if p < 5:
                pre_b1(0)

        def hook2(m, p=p):
            if p < 5:
                if m == 0:
                    pre_b2(0)
                    pre_b1(1)
                elif m == 1:
                    pre_b2(1)
                elif m == 2:
                    front_A(p + 1)
        ffn_pair(v, dsts[p], hook=hook, mid=mid, hook2=hook2)
        if p == 1:
            for nrm in range(2):
                dma("sp", GGt[nrm][:], gg_sc[nrm:nrm + 1, :].partition_broadcast(128), [f"ggsc{nrm}", f"GG{nrm}"], [f"GG{nrm}"])
        if p == 1:
            bt = bank()
            tr(bap(bt, 128)[0:32, :], nsb[:, 0:32], ["nsb"], [bn(bt)])
            cp(nsT[:], bap(bt, 128)[0:32, :], [bn(bt)], ["nsT"])
            dma("pool", ns, nsT[:], ["nsT"], ["nsdst"], is_out=True)

    S.finalize()
    with ExitStack() as es:
        esems = {e: es.enter_context(nc.semaphore(f"sem_{e}")) for e in Sched.ENGS}
        dsems = {}
        for q in ("sp", "pool"):
            for i in range(S.n_dma_sems):
                dsems[(q, i)] = es.enter_context(nc.semaphore(f"dsem_{q}_{i}"))
        block = es.enter_context(nc.Block())
        S.emit(nc, block, esems, dsems)
    return nc


def _pool_mats(L_rows, mirrored):
    wins = (2, 4, 8, 16)
    out = np.zeros((4, 256, 256), np.float32)
    for g, w in enumerate(wins):
        left = w // 2
        right = w - 1 - left
        A = np.zeros((L_rows, L_rows), np.float64)
        for t in range(L_rows):
            lo = max(t - left, 0)
            hi = min(t + right + 1, L_rows)
            A[t, lo:hi] = 1.0 / (hi - lo)
        A -= np.eye(L_rows)
        if mirrored:
            A = A[::-1, ::-1]
        for r in range(256 // L_rows):
            sl = slice(r * L_rows, (r + 1) * L_rows)
            out[g, sl, sl] = A.T
    return out


def _pp(vec, nchunk):
    return np.ascontiguousarray(np.asarray(vec, np.float32).reshape(nchunk, 128).T)


_NC_CACHE = {}


def kernel(x_prompt, x_sample, c, state_lru, c_ctx, w_mod, b_mod, norm_mix_pre, norm_mix_post,
           norm_ffn_pre, norm_ffn_post, w_in, w_pool, pool_scale, conv_w, conv_b,
           lru_w_a, lru_b_a, lru_w_x, lru_b_x, lru_lambda, w_out, w_ffn_in, w_ffn_out):
    f = lambda a: np.ascontiguousarray(np.asarray(a, np.float32))
    x_prompt, x_sample, c, state_lru, c_ctx = map(f, (x_prompt, x_sample, c, state_lru, c_ctx))
    shared = {
        "w_mod": f(w_mod[0]),
        "bmod_pp": _pp(b_mod[0], 48),
        "bmod_row": f(b_mod[0]).reshape(1, -1),
        "g_pp": np.ascontiguousarray(np.stack([_pp(norm_mix_pre[0], 8), _pp(norm_ffn_pre[0], 8)], axis=1)),
        "gpost_row": np.ascontiguousarray(np.stack([f(norm_mix_post[0]), f(norm_ffn_post[0])], axis=0)),
        "w_in": f(w_in[0]),
        "w_pool": f(w_pool[0]),
        "psc_pp": _pp(pool_scale[0], 4),
        "convb_pp": _pp(conv_b[0], 4),
        "w_out": f(w_out[0]),
        "w_ffn_in": f(w_ffn_in[0]),
        "w_ffn_out": f(w_ffn_out[0]),
        "ident": np.eye(128, dtype=np.float32),
    }
    cw = f(conv_w[0])
    zero = np.zeros((1, LW), np.float32)
    in_maps = []
    for core in range(8):
        b = core // 2
        mir = core % 2 == 1
        dP, dQ = (1, 0) if mir else (0, 1)
        xp = x_prompt[4 * core:4 * core + 4]
        xf = x_sample[b]
        if mir:
            xp = xp[:, ::-1]
            xf = xf[::-1]
            taps = np.concatenate([zero, cw[::-1]], axis=0)
        else:
            taps = np.concatenate([cw, zero], axis=0)
        m = dict(shared)
        m["xp"] = np.ascontiguousarray(xp.reshape(NPB * 256, D))
        m["xf"] = np.ascontiguousarray(xf)
        m["cvT"] = np.ascontiguousarray(np.stack([_pp(c_ctx, 8), _pp(c[b], 8)], axis=2))
        m["st0"] = np.ascontiguousarray(np.stack([_pp(state_lru[b, 0, dP], 4), _pp(state_lru[b, 0, dQ], 4)], axis=1))
        m["conv5_pp"] = np.ascontiguousarray(taps.T.reshape(4, 128, 5).transpose(1, 0, 2))
        m["gwa"] = np.ascontiguousarray(f(lru_w_a[0])[[dP, dQ]])
        m["gwx"] = np.ascontiguousarray(f(lru_w_x[0])[[dP, dQ]])
        m["gba_pp"] = np.ascontiguousarray(np.stack([_pp(lru_b_a[0, dP], 4), _pp(lru_b_a[0, dQ], 4)], axis=1))
        m["gbx_pp"] = np.ascontiguousarray(np.stack([_pp(lru_b_x[0, dP], 4), _pp(lru_b_x[0, dQ], 4)], axis=1))
        m["lam_pp"] = np.ascontiguousarray(np.stack([_pp(lru_lambda[0, dP], 4), _pp(lru_lambda[0, dQ], 4)], axis=1))
        m["pm_p"] = _pool_mats(256, mir)
        m["pm_s"] = _pool_mats(64, mir)
        in_maps.append(m)

    if "nc" not in _NC_CACHE:
        _NC_CACHE["nc"] = build_program()
    nc = _NC_CACHE["nc"]
    res = bu.run_bass_kernel_spmd(nc, in_maps, core_ids=list(range(8)))

    y_prompt = np.empty((32, 256, D), np.float32)
    y_sample = np.empty((4, 4096, D), np.float32)
    new_state = np.empty((32, 1, 2, LW), np.float32)
    for core in range(8):
        r = res.results[core]
        b = core // 2
        mir = core % 2 == 1
        dP, dQ = (1, 0) if mir else (0, 1)
        ypc = np.asarray(r["yp"], np.float32).reshape(4, 256, D)
        ysc = np.asarray(r["ys"], np.float32)
        nsc = np.asarray(r["ns"], np.float32).reshape(4, 2, 4 * 128)
        if mir:
            y_prompt[4 * core:4 * core + 4] = ypc[:, ::-1]
            y_sample[b, 2048:] = ysc[::-1]
        else:
            y_prompt[4 * core:4 * core + 4] = ypc
            y_sample[b, :2048] = ysc
        new_state[4 * core:4 * core + 4, 0, dP] = nsc[:, 0]
        new_state[4 * core:4 * core + 4, 0, dQ] = nsc[:, 1]
    return (y_prompt, y_sample, new_state)
```

```python
import numpy as np
import concourse.bass as bass
import concourse.mybir as mybir
import concourse.bass_utils as bu
from contextlib import ExitStack

F32 = mybir.dt.float32
BF16 = mybir.dt.bfloat16
AF = mybir.ActivationFunctionType
ALU = mybir.AluOpType

D = 1024
DFF = 2816
NF = 22
LW = 512
NPB = 4
NSB = 8
NQB = 16
EPS = 1e-6


class _Op:
    __slots__ = ("id", "eng", "fn", "cdeps", "ddeps", "sig", "sigval", "dma", "dsem", "dval", "prewait")


class Sched:
    ENGS = ("pe", "act", "dve", "pool", "sp")

    def __init__(self, n_dma_sems=12):
        self.ops = []
        self.by_eng = {e: [] for e in self.ENGS}
        self.lastw = {}
        self.readers = {}
        self.n_dma_sems = n_dma_sems
        self.dma_cnt = {"sp": 0, "pool": 0, "act": 0}
        self.dma_tot = {}
        self.arena_barrier = {}
        self.arena_cur = {}
        self.out_dmas = []

    def arena_switch(self):
        self.arena_barrier = dict(self.arena_cur)
        self.arena_cur = {}

    def op(self, eng, fn, reads=(), writes=(), dma=False, arena=False, out=False):
        o = _Op()
        o.id = len(self.ops)
        o.eng = eng
        o.fn = fn
        o.dma = dma
        o.sig = False
        o.sigval = None
        o.prewait = None
        deps = set()
        for b in reads:
            w = self.lastw.get(b)
            if w is not None:
                deps.add(w)
        for b in writes:
            w = self.lastw.get(b)
            if w is not None:
                deps.add(w)
            deps.update(self.readers.get(b, ()))
        if arena:
            deps.update(self.arena_barrier.values())
        cd = {}
        dd = set()
        for d in deps:
            od = self.ops[d]
            if od.dma:
                dd.add(d)
            else:
                if eng == "pe" and od.eng == "pe":
                    continue
                if od.eng not in cd or cd[od.eng] < d:
                    cd[od.eng] = d
        o.cdeps = cd
        o.ddeps = dd
        if dma:
            k = self.dma_cnt[eng]
            self.dma_cnt[eng] = k + 1
            key = (eng, k % self.n_dma_sems)
            prev = self.dma_tot.get(key, 0)
            o.dsem = key
            o.dval = prev + 16
            o.prewait = prev
            self.dma_tot[key] = prev + 16
            if out:
                self.out_dmas.append(o.id)
        for b in writes:
            self.lastw[b] = o.id
            self.readers[b] = []
        for b in reads:
            self.readers.setdefault(b, []).append(o.id)
        if arena:
            self.arena_cur[eng] = o.id
        self.ops.append(o)
        self.by_eng[eng].append(o)
        return o

    def finalize(self):
        f = _Op()
        f.id = len(self.ops)
        f.eng = "sp"
        f.fn = None
        f.dma = False
        f.sig = False
        f.sigval = None
        f.prewait = None
        f.cdeps = {}
        f.ddeps = set(self.out_dmas)
        self.ops.append(f)
        self.by_eng["sp"].append(f)
        for o in self.ops:
            for d in o.cdeps.values():
                self.ops[d].sig = True
        cnt = {e: 0 for e in self.ENGS}
        for o in self.ops:
            if o.sig:
                cnt[o.eng] += 1
                o.sigval = cnt[o.eng]

    def emit(self, nc, block, esems, dsems):
        ops = self.ops

        def run(engname, e):
            waited = {}

            def w(sem, key, val):
                if val <= 0:
                    return
                if waited.get(key, 0) >= val:
                    return
                waited[key] = val
                e.wait_ge(sem, val)

            for o in self.by_eng[engname]:
                for en, d in o.cdeps.items():
                    w(esems[en], ("e", en), ops[d].sigval)
                for d in o.ddeps:
                    od = ops[d]
                    w(dsems[od.dsem], ("d", od.dsem), od.dval)
                if o.dma:
                    w(dsems[o.dsem], ("d", o.dsem), o.prewait)
                if o.fn is None:
                    continue
                ins = o.fn(e)
                if o.dma:
                    ins.then_inc(dsems[o.dsem], 16)
                elif o.sig:
                    ins.then_inc(esems[engname], 1)

        @block.tensor
        def _(e):
            run("pe", e)

        @block.scalar
        def _(e):
            run("act", e)

        @block.vector
        def _(e):
            run("dve", e)

        @block.gpsimd
        def _(e):
            run("pool", e)

        @block.sync
        def _(e):
            run("sp", e)


def build_program():
    nc = bass.Bass("TRN2", target_bir_lowering=False)
    S = Sched()

    def din(name, shape, dt=F32):
        return nc.dram_tensor(name, list(shape), dt, kind="ExternalInput").ap()

    def dout(name, shape, dt=F32):
        return nc.dram_tensor(name, list(shape), dt, kind="ExternalOutput").ap()

    xp = din("xp", [NPB * 256, D])
    xf = din("xf", [4096, D])
    cvT = din("cvT", [128, 8, 2])
    st0 = din("st0", [128, 2, 4])
    w_mod = din("w_mod", [D, 6 * D])
    bmod_pp = din("bmod_pp", [128, 48])
    bmod_row = din("bmod_row", [1, 6 * D])
    g_pp = din("g_pp", [128, 2, 8])
    gpost_row = din("gpost_row", [2, D])
    w_in = din("w_in", [D, 1536])
    w_pool = din("w_pool", [4, 128, 128])
    psc_pp = din("psc_pp", [128, 4])
    conv5_pp = din("conv5_pp", [128, 4, 5])
    convb_pp = din("convb_pp", [128, 4])
    gwa = din("gwa", [2, 8, 64, 64])
    gwx = din("gwx", [2, 8, 64, 64])
    gba_pp = din("gba_pp", [128, 2, 4])
    gbx_pp = din("gbx_pp", [128, 2, 4])
    lam_pp = din("lam_pp", [128, 2, 4])
    w_out = din("w_out", [D, D])
    w_ffn_in = din("w_ffn_in", [D, 2 * DFF])
    w_ffn_out = din("w_ffn_out", [DFF, D])
    pm_p = din("pm_p", [4, 256, 256])
    pm_s = din("pm_s", [4, 256, 256])
    ident_d = din("ident", [128, 128])
    yp = dout("yp", [NPB * 256, D])
    ys = dout("ys", [NSB * 256, D])
    ns = dout("ns", [32, 128])
    sc_in = nc.dram_tensor("sc_in", [D, 2 * DFF], BF16, kind="Internal").ap()
    sc_out = nc.dram_tensor("sc_out", [DFF, D], BF16, kind="Internal").ap()

    def sb(name, shape, dt=F32):
        return nc.alloc_sbuf_tensor(name, list(shape), dt)

    w_in_sb = sb("w_in_sb", [128, 8, 1536], BF16)
    w_out_sb = sb("w_out_sb", [128, 8, D], BF16)
    gw_sb = sb("gw_sb", [128, 2, 2, 4, 128], BF16)
    pm_one = sb("pm_sb", [128, 4, 2, 256], BF16)
    wpool_sb = sb("wpool_sb", [128, 4, 128], BF16)
    ident = sb("ident_sb", [128, 128], F32)
    GGt = [sb(f"GG{n}", [128, D], F32) for n in range(2)]
    gg_sc = nc.dram_tensor("gg_sc", [2, D], F32, kind="Internal").ap()
    Gp = sb("Gp", [128, 2, 2, 8], F32)
    Sp = sb("Sp", [128, 2, 2, 8], F32)
    cv_sb = sb("cv_sb", [128, 8, 2], F32)
    sc_bf = sb("sc_bf", [128, 8, 2], BF16)
    bmodpp_sb = sb("bmodpp_sb", [128, 48], F32)
    gpp_sb = sb("gpp_sb", [128, 2, 8], F32)
    psc_sb = sb("psc_sb", [128, 4], F32)
    conv5_sb = sb("conv5_sb", [128, 4, 5], F32)
    convb_sb = sb("convb_sb", [128, 4], F32)
    gba_sb = sb("gba_sb", [128, 2, 4], F32)
    gbx_sb = sb("gbx_sb", [128, 2, 4], F32)
    lam_sb = sb("lam_sb", [128, 2, 4], F32)
    hba_sb = sb("hba_sb", [128, 2, 4], F32)
    hbx_sb = sb("hbx_sb", [128, 2, 4], F32)
    cL_sb = sb("cL_sb", [128, 2, 4], F32)
    cLh_sb = sb("cLh_sb", [128, 2, 4], F32)
    st_sb = sb("st_sb", [128, 2, 4], F32)
    st0_sb = sb("st0_sb", [128, 2, 4], F32)
    mhalf = sb("mhalf", [128, 8], F32)
    nsb = sb("nsb", [128, 32], F32)
    nsT = sb("nsT", [32, 128], F32)
    ss = sb("ss", [128, 16], F32)
    var = sb("var", [128, 16], F32)
    rstd = sb("rstd", [128, 16], F32)
    uhead = sb("uhead", [128, 4, 9, 2], F32)
    utail = sb("utail", [128, 4, 2], F32)
    HQ = sb("HQ", [128, 4, 2048], BF16)
    X1 = sb("X1", [128, 4, 1024], F32)
    xs = [sb(f"xs{i}", [128, D], F32) for i in range(2)]
    tmpr = [sb(f"tmpr{i}", [128, D], BF16) for i in range(2)]
    hT = sb("hT", [128, 8, 512], BF16)
    wfi = [sb(f"wfi{i}", [128, 8, 2, 128], BF16) for i in range(2)]
    wfo = [sb(f"wfo{i}", [128, NF, 128], BF16) for i in range(2)]
    xq = [sb(f"xq{i}", [128, D], F32) for i in range(2)]
    xin = [xq[0], xq[1], sb("xin2", [128, D], F32), sb("xin3", [128, D], F32)]
    xin_n = ["xq0", "xq1", "xin2", "xin3"]

    ARENA_BYTES = 42 * 1024
    arena = sb("arena", [128, ARENA_BYTES // 4], F32)
    _off = [0]

    def carve(shape, dt, reset=False):
        if reset:
            _off[0] = 0
        n = int(np.prod(shape))
        nb = n * (2 if dt == BF16 else 4)
        nb_al = (nb + 63) // 64 * 64
        o = _off[0]
        assert o + nb_al <= ARENA_BYTES, (o, nb_al)
        _off[0] = o + nb_al
        a = arena[:, o // 4:(o + nb) // 4]
        if dt == BF16:
            a = a.bitcast(BF16)
        if len(shape) == 2:
            return a.rearrange("p (a b) -> p a b", a=shape[0], b=shape[1])
        if len(shape) == 3:
            return a.rearrange("p (a b c) -> p a b c", a=shape[0], b=shape[1], c=shape[2])
        return a

    upool = sb("upool", [128, 2, 512], BF16)
    mT = sb("mT", [128, 4, 256], BF16)
    ypT2 = [sb("ypT0", [128, 4, 256], BF16), None]
    gel2 = [sb("gel0", [128, 4, 256], BF16), None]
    ubuf_m = [sb("ubm0", [128, 4, 260], F32), None]
    xba = sb("xba", [128, 4, 256], F32)
    xbba = sb("xbba", [128, 4, 256], BF16)
    NCH = 4
    TT = carve([4, 2, 256], F32, reset=True)
    T3a = carve([4, 256], F32)
    hOa = carve([4, 256], F32)
    T1v = TT[:, :, 0, :]
    T2v = TT[:, :, 1, :]
    lru_end = _off[0]
    ypT2[1] = carve([4, 256], BF16)
    gel2[1] = carve([4, 256], BF16)
    ylT = carve([4, 256], BF16)
    ubuf_m[1] = carve([4, 260], F32)
    _off[0] = lru_end
    NUQ = 5
    ubuf_q = [carve([4, 260], F32) for _ in range(NUQ)]
    actT = carve([NF, 512], BF16, reset=True)
    ffT = carve([8, 512], F32)
    sg = [carve([512], F32) for _ in range(2)]

    PS = [nc.alloc_psum_tensor(f"ps{i}", [128, 1024], F32) for i in range(4)]
    bank_ctr = [0]

    def _bank_free(b):
        nm = f"bank{b}"
        assert S.lastw.get(nm) is None or len(S.readers.get(nm, ())) > 0, f"PSUM bank {b} reused before evacuation"

    def bank():
        b = bank_ctr[0] % 8
        bank_ctr[0] += 1
        _bank_free(b)
        return b

    def bank2():
        if bank_ctr[0] % 2:
            bank_ctr[0] += 1
        b = bank_ctr[0] % 8
        bank_ctr[0] += 2
        _bank_free(b)
        _bank_free(b + 1)
        return b

    def bap(b, w=512):
        return PS[b // 2][:, (b % 2) * 512:(b % 2) * 512 + w]

    def bn(b):
        return f"bank{b}"

    def mm(out, lhsT, rhs, start, stop, reads, writes, arena=False):
        S.op("pe", lambda e: e.matmul(out, lhsT, rhs, start=start, stop=stop), reads, writes, arena=arena)

    def tr(out, in_, reads, writes, arena=False):
        S.op("pe", lambda e: e.transpose(out, in_, ident[:]), list(reads) + ["ident"], writes, arena=arena)

    def act(out, in_, func, reads, writes, bias=None, scale=None, accum=None, arena=False):
        kw = {}
        if bias is not None:
            kw["bias"] = bias
        if scale is not None:
            kw["scale"] = scale
        if accum is not None:
            kw["accum_out"] = accum
        S.op("act", lambda e: e.activation(out, in_, func, **kw), reads, writes, arena=arena)

    def ts(out, in0, s1, s2, op0, op1, reads, writes, eng="dve", arena=False):
        if op1 is None:
            S.op(eng, lambda e: e.tensor_scalar(out, in0, s1, None, op0), reads, writes, arena=arena)
        else:
            S.op(eng, lambda e: e.tensor_scalar(out, in0, s1, s2, op0, op1), reads, writes, arena=arena)

    def tt(out, in0, in1, op, reads, writes, eng="dve", arena=False):
        S.op(eng, lambda e: e.tensor_tensor(out, in0, in1, op), reads, writes, arena=arena)

    def stt(out, in0, sc, in1, op0, op1, reads, writes, arena=False):
        S.op("dve", lambda e: e.scalar_tensor_tensor(out, in0, sc, in1, op0, op1), reads, writes, arena=arena)

    def cp(out, in_, reads, writes, eng="dve", arena=False):
        if eng == "act":
            S.op("act", lambda e: e.activation(out, in_, AF.Copy), reads, writes, arena=arena)
        else:
            S.op(eng, lambda e: e.tensor_copy(out, in_), reads, writes, arena=arena)

    def mset(ap, val, writes, eng="pool", arena=False):
        S.op(eng, lambda e: e.memset(ap, val), (), writes, arena=arena)

    def dma(q, out, in_, reads, writes, arena=False, is_out=False):
        S.op(q, lambda e: e.dma_start(out=out, in_=in_), reads, writes, dma=True, arena=arena, out=is_out)

    for dst, src, nm in [(ident[:], ident_d, "ident"), (cv_sb[:], cvT, "cv"), (bmodpp_sb[:], bmod_pp, "bmodpp"),
                         (gpp_sb[:], g_pp, "gpp"), (psc_sb[:], psc_pp, "psc"), (conv5_sb[:], conv5_pp, "conv5"),
                         (convb_sb[:], convb_pp, "convb"), (gba_sb[:], gba_pp, "gba"), (gbx_sb[:], gbx_pp, "gbx"),
                         (lam_sb[:], lam_pp, "lam"), (st0_sb[:], st0, "st0")]:
        dma("sp", dst, src, (), [nm])
    mset(mhalf[:], -0.5, ["mhalf"])
    mset(gw_sb[:], 0.0, ["gw"])
    mset(nsb[:], 0.0, ["nsb"])

    act(sc_bf[:], cv_sb[:], AF.Silu, ["cv"], ["scbf"])
    def screp(v, kc):
        return hT[:, kc, 256 + v * 128:256 + (v + 1) * 128]
    for v in range(2):
        for kc in range(8):
            cp(screp(v, kc), sc_bf[:, kc, v:v + 1].to_broadcast([128, 128]), ["scbf"], ["hT1"])

    ts(hba_sb[:], gba_sb[:], 0.5, None, ALU.mult, None, ["gba"], ["hba"])
    ts(hbx_sb[:], gbx_sb[:], 0.5, None, ALU.mult, None, ["gbx"], ["hbx"])
    act(cL_sb[:], lam_sb[:], AF.Exp, ["lam"], ["cL"], scale=-1.0)
    ts(cL_sb[:], cL_sb[:], 1.0, None, ALU.add, None, ["cL"], ["cL"])
    act(cLh_sb[:], cL_sb[:], AF.Ln, ["cL"], ["cLh"])
    ts(cL_sb[:], cLh_sb[:], -8.0, None, ALU.mult, None, ["cLh"], ["cL"])
    ts(cLh_sb[:], cLh_sb[:], -4.0, None, ALU.mult, None, ["cLh", "cL"], ["cLh"])

    gstage = xin[3][:].rearrange("p (a c j) -> p a c j", a=4, c=4, j=64)
    for d in range(2):
        for ax, src in enumerate((gwa, gwx)):
            dma("sp", gstage[:, d * 2 + ax, :, :], src[d].rearrange("(c hh) i j -> (hh i) c j", hh=2), (), [f"gst{d}{ax}"])
    for d in range(2):
        for ax in range(2):
            for hh in range(2):
                cp(gw_sb[hh * 64:(hh + 1) * 64, d, ax, :, hh * 64:(hh + 1) * 64],
                   gstage[hh * 64:(hh + 1) * 64, d * 2 + ax, :, :], [f"gst{d}{ax}", "gw", "xin3"], [f"gwd{d}{ax}"])

    wmod_v = w_mod.rearrange("(kc p) n -> p kc n", p=128)
    X1_names = [f"X1_{q}" for q in range(4)]
    stage_X1 = X1[:].bitcast(BF16).rearrange("p a (b c) -> p (a b) c", b=2, c=1024)
    stage_HQ = HQ[:].rearrange("p a (b c) -> p (a b) c", b=2, c=1024)

    def mod_section(sec, wbuf, wnames):
        dma("pool", wbuf, wmod_v[:, :, sec * D:(sec + 1) * D], (), wnames)
        if sec in (0, 1, 3, 4):
            b = bank()
            pv = bap(b, 16).rearrange("p (c v) -> p c v", c=8, v=2)
            for ch in range(8):
                for kc in range(8):
                    mm(pv[:, ch, :], wbuf[:, kc, ch * 128:(ch + 1) * 128], sc_bf[:, kc, :], kc == 0, kc == 7,
                       list(wnames) + ["scbf"], [bn(b)])
            nrm = 0 if sec < 3 else 1
            for v in range(2):
                if sec in (0, 3):
                    tt(Sp[:, nrm, v, :], pv[:, :, v], bmodpp_sb[:, sec * 8:(sec + 1) * 8], ALU.add,
                       [bn(b), "bmodpp"], ["Sp"])
                else:
                    tt(Gp[:, nrm, v, :], pv[:, :, v], bmodpp_sb[:, sec * 8:(sec + 1) * 8], ALU.add,
                       [bn(b), "bmodpp"], ["Gp"])
                    stt(Gp[:, nrm, v, :], Gp[:, nrm, v, :], 1.0, gpp_sb[:, nrm, :], ALU.add, ALU.mult,
                        ["Gp", "gpp"], ["Gp"])
        else:
            nrm = 0 if sec == 2 else 1
            gprow = xin[2][:]
            dma("sp", gprow, gpost_row[nrm:nrm + 1, :].partition_broadcast(128), (), ["xin2"])
            for v in (1, 0):
                dma("sp", GGt[nrm][:], bmod_row[0:1, sec * D:(sec + 1) * D].partition_broadcast(128), (), [f"GG{nrm}"])
                for half in range(2):
                    b = bank()
                    for kc in range(8):
                        mm(bap(b), screp(v, kc), wbuf[:, kc, half * 512:(half + 1) * 512], kc == 0, kc == 7,
                           list(wnames) + ["hT1"], [bn(b)])
                    gsl = GGt[nrm][:, half * 512:(half + 1) * 512]
                    tt(gsl, bap(b), gsl, ALU.add, [bn(b), f"GG{nrm}"], [f"GG{nrm}"])
                    tt(gsl, gsl, gprow[:, half * 512:(half + 1) * 512], ALU.mult, [f"GG{nrm}", "xin2"], [f"GG{nrm}"])
                if v == 1:
                    dma("sp", gg_sc[nrm:nrm + 1, :], GGt[nrm][0:1, :], [f"GG{nrm}"], [f"ggsc{nrm}"])

    mod_section(0, stage_X1, X1_names)
    mod_section(1, stage_HQ, ["HQ"])
    dma("pool", w_in_sb[:], w_in.rearrange("(kc p) n -> p kc n", p=128), (), ["w_in"])

    deferred = []
    for sec in (2, 3, 4, 5):
        deferred.append(lambda sec=sec: mod_section(sec, stage_X1, X1_names))
    deferred.append(lambda: dma("pool", pm_one[:], pm_p.rearrange("g (sc p) t -> p g sc t", p=128), (), ["pm"]))
    deferred.append(lambda: (dma("pool", wpool_sb[:], w_pool.rearrange("g c d -> c g d"), (), ["wpool"]),
                             dma("pool", w_out_sb[:], w_out.rearrange("(kc p) n -> p kc n", p=128), (), ["w_out"])))
    for i in range(4):
        deferred.append(lambda i=i: dma("pool", sc_in[:, i * 1408:(i + 1) * 1408],
                                        w_ffn_in[:, i * 1408:(i + 1) * 1408], (), [f"sc_in{i}"]))
    for i in range(2):
        deferred.append(lambda i=i: dma("pool", sc_out[i * 1408:(i + 1) * 1408, :],
                                        w_ffn_out[i * 1408:(i + 1) * 1408, :], (), [f"sc_out{i}"]))

    S.arena_switch()

    ss_ctr = [0]

    def norm_stats(src_tiles, src_names):
        cols = []
        for t in range(2):
            c = ss_ctr[0] % 16
            ss_ctr[0] += 1
            cols.append(c)
            act(xs[t % 2][:], src_tiles[t], AF.Square, [src_names[t]], [f"xs{t % 2}", f"ss{c}"], accum=ss[:, c:c + 1])
            ts(var[:, c:c + 1], ss[:, c:c + 1], 1.0 / D, EPS, ALU.mult, ALU.add, [f"ss{c}"], [f"var{c}"], eng="pool")
            tt(rstd[:, c:c + 1], var[:, c:c + 1], mhalf[:, 0:1], ALU.pow, [f"var{c}", "mhalf"], [f"rstd{c}"], eng="pool")
        return cols

    def norm_transpose(src_tiles, src_names, nrm, v, ntile, hsel=0, evac="act"):
        cols = norm_stats(src_tiles, src_names)
        norm_apply(src_tiles, src_names, cols, nrm, v, hsel, evac)

    def norm_scale(src_tiles, src_names, cols):
        for t in range(2):
            c = cols[t]
            act(xs[t % 2][:], src_tiles[t], AF.Copy, [src_names[t], f"rstd{c}"], [f"xs{t % 2}"], scale=rstd[:, c:c + 1])

    def norm_tr():
        bks = [bank() for _ in range(4)]
        for t in range(2):
            xsb = xs[t % 2]
            for kp in range(4):
                for k2 in range(2):
                    kc = kp * 2 + k2
                    o = (k2 * 2 + t) * 128
                    tr(bap(bks[kp])[:, o:o + 128], xsb[:, kc * 128:(kc + 1) * 128], [f"xs{t % 2}"], [bn(bks[kp])])
        return bks

    def norm_scale_tr(src_tiles, src_names, cols):
        bks = [bank() for _ in range(4)]
        for t in range(2):
            c = cols[t]
            xsb = xs[t % 2]
            act(xsb[:], src_tiles[t], AF.Copy, [src_names[t], f"rstd{c}"], [f"xs{t % 2}"], scale=rstd[:, c:c + 1])
            for kp in range(4):
                for k2 in range(2):
                    kc = kp * 2 + k2
                    o = (k2 * 2 + t) * 128
                    tr(bap(bks[kp])[:, o:o + 128], xsb[:, kc * 128:(kc + 1) * 128], [f"xs{t % 2}"], [bn(bks[kp])])
        return bks

    def norm_evac(bks, nrm, v, hsel=0, evac="act"):
        base = hsel * 256
        for kp in range(4):
            for k2 in range(2):
                kc = kp * 2 + k2
                if evac == "act":
                    act(hT[:, kc, base:base + 256], bap(bks[kp])[:, k2 * 256:(k2 + 1) * 256], AF.Identity,
                        [bn(bks[kp]), "Gp", "Sp"], [f"hT{hsel}"], bias=Sp[:, nrm, v, kc:kc + 1], scale=Gp[:, nrm, v, kc:kc + 1])
                else:
                    ts(hT[:, kc, base:base + 256], bap(bks[kp])[:, k2 * 256:(k2 + 1) * 256],
                       Gp[:, nrm, v, kc:kc + 1], Sp[:, nrm, v, kc:kc + 1], ALU.mult, ALU.add,
                       [bn(bks[kp]), "Gp", "Sp"], [f"hT{hsel}"])

    def norm_apply(src_tiles, src_names, cols, nrm, v, hsel=0, evac="act"):
        norm_evac(norm_scale_tr(src_tiles, src_names, cols), nrm, v, hsel, evac)

    def xpart_mm(hsel=0):
        hb = hsel * 256
        bks = []
        for cp_ in range(2):
            b = bank()
            bks.append(b)
            for c2 in range(2):
                cc = cp_ * 2 + c2
                for kc in range(8):
                    mm(bap(b)[:, c2 * 256:(c2 + 1) * 256], w_in_sb[:, kc, 512 + cc * 128:512 + (cc + 1) * 128],
                       hT[:, kc, hb:hb + 256], kc == 0, kc == 7, ["w_in", f"hT{hsel}"], [bn(b)])
        return bks

    def xpart_evac(bks, ub, ubname, ar=True):
        for cp_, b in enumerate(bks):
            cp(ub[:, cp_ * 2:cp_ * 2 + 2, 2:258], bap(b).rearrange("p (c n) -> p c n", c=2), [bn(b)], [ubname],
               eng="act", arena=ar)

    def xpart(n, ub, ubname, hsel=0, ar=True):
        xpart_evac(xpart_mm(hsel), ub, ubname, ar)

    XBN = [f"xb{cc}" for cc in range(4)]
    XBSET = [(xba[:], xbba[:], XBN, "xbb"),
             (ubuf_m[0][:, :, 0:256], ypT2[0][:], ["ubm0"], "ypT0")]

    def conv_all(ub, ubname, ar=True, xset=0):
        xf32, xb16, xn, xbn = XBSET[xset]
        for cc in range(4):
            nm = [xn[cc]] if len(xn) == 4 else xn
            ts(xf32[:, cc, :], ub[:, cc, 0:256], conv5_sb[:, cc, 0:1], convb_sb[:, cc:cc + 1], ALU.mult, ALU.add,
               [ubname, "conv5", "convb"], nm, arena=ar)
            for k in range(1, 5):
                stt(xf32[:, cc, :], ub[:, cc, k:k + 256], conv5_sb[:, cc, k:k + 1], xf32[:, cc, :], ALU.mult, ALU.add,
                    [ubname, "conv5"] + nm, nm, arena=ar)
        cp(xb16, xf32, xn, [xbn], arena=ar)

    ALLC = list(range(4))

    T1n = [f"T1{i}" for i in ALLC]
    T2n = [f"T2{i}" for i in ALLC]
    T3n_ = [f"T3{i}" for i in ALLC]
    hOn_ = [f"hO{i}" for i in ALLC]

    def wave_gates(chains, xset=0):
        xb16, xbn = XBSET[xset][1], XBSET[xset][3]
        banks = []
        for i, c in enumerate(chains):
            b = bank()
            banks.append(b)
            cc, d = c["cc"], c["d"]
            mm(bap(b)[:, 0:256], gw_sb[:, d, 0, cc, :], xb16[:, cc, :], True, True,
               ["gw", f"gwd{d}0", xbn], [bn(b)], arena=True)
            mm(bap(b)[:, 256:512], gw_sb[:, d, 1, cc, :], xb16[:, cc, :], True, True,
               ["gw", f"gwd{d}1", xbn], [bn(b)], arena=True)
        return banks

    def wave_act1(chains, banks, xsel):
        for i, c in enumerate(chains):
            cc, d = c["cc"], c["d"]
            b = banks[i]
            act(TT[:, i, 0, :], bap(b)[:, 0:256], AF.Tanh, [bn(b), "hba"], [f"T1{i}"], bias=hba_sb[:, d, cc:cc + 1], scale=0.5, arena=True)
            act(TT[:, i, 1, :], bap(b)[:, 256:512], AF.Tanh, [bn(b), "hbx"], [f"T2{i}"], bias=hbx_sb[:, d, cc:cc + 1], scale=0.5, arena=True)
        for i, c in enumerate(chains):
            cc, d = c["cc"], c["d"]
            act(T3a[:, i, :], TT[:, i, 0, :], AF.Exp, [f"T1{i}", "cLh"], [f"T3{i}"], bias=cLh_sb[:, d, cc:cc + 1],
                scale=cLh_sb[:, d, cc:cc + 1], arena=True)
        act(T1v, T3a[:], AF.Square, T3n_, T1n, arena=True)
        ts(T1v, T1v, 1.0, -1.0, ALU.min, ALU.mult, T1n, T1n, arena=True)
        for (t2view, xbview, names) in xsel:
            stt(t2view, t2view, 1.0, xbview, ALU.add, ALU.mult, T2n + names, T2n, arena=True)

    def wave_fin(chains):
        act(T1v, T1v, AF.Sqrt, T1n, T1n, bias=1.0, scale=1.0, arena=True)
        stt(T2v, T2v, 0.5, T1v, ALU.mult, ALU.mult, T2n + T1n, T2n, arena=True)
        for i, c in enumerate(chains):
            init_ap, init_names = c["init"], c["init_names"]
            if c["rev"]:
                S.op("dve", lambda e, i=i, init_ap=init_ap: e.tensor_tensor_scan(
                    hOa[:, i, ::-1], T3a[:, i, ::-1], TT[:, i, 1, ::-1], init_ap, ALU.mult, ALU.add),
                     [f"T3{i}", f"T2{i}"] + list(init_names), [f"hO{i}"], arena=True)
            else:
                S.op("dve", lambda e, i=i, init_ap=init_ap: e.tensor_tensor_scan(
                    hOa[:, i, :], T3a[:, i, :], TT[:, i, 1, :], init_ap, ALU.mult, ALU.add),
                     [f"T3{i}", f"T2{i}"] + list(init_names), [f"hO{i}"], arena=True)
        return T3n_, hOn_

    def lru_wave(chains, xsel):
        banks = wave_gates(chains)
        wave_act1(chains, banks, xsel)
        return wave_fin(chains)

    def q_load_stats(j):
        tiles = []
        names = []
        for t in range(2):
            i = (2 * j + t) % 2
            dma("sp", xq[i][:], xf[j * 256 + t * 128:j * 256 + (t + 1) * 128, :], (), [f"xq{i}"])
            tiles.append(xq[i][:])
            names.append(f"xq{i}")
        return tiles, names, norm_stats(tiles, names)

    def q_front_2(j, bks):
        ub = ubuf_q[j % NUQ]
        ubn = f"ubq{j % NUQ}"
        xpart_evac(bks, ub, ubn)
        if j <= 8:
            cp(uhead[:, :, j, :], ub[:, :, 2:4], [ubn], ["uhead"], eng="pool", arena=True)
        if j == NQB - 1:
            mset(ub[:, :, 258:260], 0.0, [ubn], eng="pool", arena=True)
        else:
            up = ubuf_q[(j + 1) % NUQ]
            upn = f"ubq{(j + 1) % NUQ}"
            cp(ub[:, :, 258:260], up[:, :, 2:4], [upn], [ubn], eng="pool", arena=True)
            cp(up[:, :, 0:2], ub[:, :, 256:258], [ubn], [upn], eng="pool", arena=True)
        if j == 0:
            mset(ub[:, :, 0:2], 0.0, [ubn], eng="pool", arena=True)

    def q_chains():
        return [dict(cc=cc, d=1, rev=True, init=st_sb[:, 1, cc:cc + 1], init_names=["st"]) for cc in range(4)]

    def q_back_fin(j, chains):
        T3n, hOn = wave_fin(chains)
        cp(st_sb[:, 1, :], hOa[:, :, 0], hOn, ["st"], arena=True)
        if j < NSB:
            cp(HQ[:, :, j * 256:(j + 1) * 256], hOa[:], hOn, ["HQ"], arena=True)

    q_pend = []
    q_gates = []

    def q_iter(jf, jc, jw):
        ok = lambda x: x is not None and 0 <= x < NQB
        chains = q_chains() if ok(jw) else None
        gbanks = q_gates.pop() if ok(jw) else None
        if ok(jc):
            conv_all(ubuf_q[jc % NUQ], f"ubq{jc % NUQ}", xset=jc % 2)
        if ok(jw):
            xs_ = XBSET[jw % 2]
            wave_act1(chains, gbanks, [(T2v, xs_[0], xs_[2])])
        if q_pend:
            q_front_2(*q_pend.pop())
        if deferred and (jf is None or jf <= NQB - 2):
            deferred.pop(0)()
        if ok(jf):
            tiles, names, cols = q_load_stats(jf)
            tbanks = norm_scale_tr(tiles, names, cols)
        if ok(jw):
            q_back_fin(jw, chains)
        if ok(jf):
            norm_evac(tbanks, 0, 1, 0)
        if ok(jc):
            q_gates.append(wave_gates(q_chains(), xset=jc % 2))
        if ok(jf):
            q_pend.append((jf, xpart_mm(0)))

    pre_state = {}

    def pre_a(src, v, hs):
        tiles = []
        names = []
        for t in range(2):
            i = 2 * hs + t
            dma("sp", xin[i][:], src[t * 128:(t + 1) * 128, :], (), [xin_n[i]])
            tiles.append(xin[i][:])
            names.append(xin_n[i])
        pre_state[hs] = (tiles, names, norm_stats(tiles, names), v)

    def pre_b(hs):
        tiles, names, cols, v = pre_state.pop(hs)
        norm_apply(tiles, names, cols, 0, v, hs)

    def pre_b1(hs):
        tiles, names, cols, v = pre_state[hs]
        norm_scale(tiles, names, cols)

    def pre_b2(hs):
        tiles, names, cols, v = pre_state.pop(hs)
        norm_evac(norm_tr(), 0, v, hs)

    def mixer_front(kind, j, v, hs):
        ypT, gel = ypT2[hs], gel2[hs]
        hb = hs * 256
        ar = (hs == 1)
        for t in range(2):
            b = bank()
            for kc in range(8):
                mm(bap(b), hT[:, kc, hb + t * 128:hb + (t + 1) * 128], w_in_sb[:, kc, 0:512], kc == 0, kc == 7,
                   [f"hT{hs}", "w_in"], [bn(b)])
            cp(upool[:, t, :], bap(b), [bn(b)], ["upool"], eng="act")
        for cc in range(4):
            b = bank()
            for kc in range(8):
                mm(bap(b, 256), w_in_sb[:, kc, 1024 + cc * 128:1024 + (cc + 1) * 128], hT[:, kc, hb:hb + 256], kc == 0, kc == 7,
                   ["w_in", f"hT{hs}"], [bn(b)])
            act(gel[:, cc, :], bap(b, 256), AF.Gelu_apprx_tanh, [bn(b)], [f"gel{hs}"], arena=ar)
        ub = ubuf_m[hs]
        ubn = f"ubm{hs}"
        xpart(256, ub, ubn, hsel=hs, ar=ar)
        for g in range(4):
            b = bank()
            for t in range(2):
                mm(bap(b, 256), upool[:, t, g * 128:(g + 1) * 128], pm_one[:, g, t, :], t == 0, t == 1,
                   ["upool", "pm"], [bn(b)])
            cp(mT[:, g, :], bap(b, 256), [bn(b)], ["mT"])
        for g in range(4):
            b = bank()
            mm(bap(b, 256), wpool_sb[:, g, :], mT[:, g, :], True, True, ["wpool", "mT"], [bn(b)])
            ts(ypT[:, g, :], bap(b, 256), psc_sb[:, g:g + 1], None, ALU.mult, None, [bn(b), "psc"], [f"ypT{hs}"], arena=ar)
        if kind == 0 or j == 0:
            mset(ub[:, :, 0:2], 0.0, [ubn], eng="pool", arena=ar)
        else:
            cp(ub[:, :, 0:2], utail[:], ["utail"], [ubn], eng="pool", arena=ar)
        if kind == 0:
            mset(ub[:, :, 258:260], 0.0, [ubn], eng="pool", arena=ar)
        else:
            cp(ub[:, :, 258:260], uhead[:, :, j + 1, :], ["uhead"], [ubn], eng="pool", arena=ar)
            cp(utail[:], ub[:, :, 256:258], [ubn], ["utail"], eng="pool", arena=ar)

    def mixer_tail_a(hs):
        conv_all(ubuf_m[hs], f"ubm{hs}", ar=(hs == 1))

    def mixer_tail(kind, j, slot0, v, hs):
        ypT, gel = ypT2[hs], gel2[hs]
        if kind == 0:
            col = j * 8
            for half in range(2):
                cA, cB = 2 * half, 2 * half + 1
                chains = [dict(cc=cA, d=0, rev=False, init=0.0, init_names=[]),
                          dict(cc=cA, d=1, rev=True, init=0.0, init_names=[]),
                          dict(cc=cB, d=0, rev=False, init=0.0, init_names=[]),
                          dict(cc=cB, d=1, rev=True, init=0.0, init_names=[])]
                xsel = [(TT[:, dd::2, 1, :], xba[:, cA:cB + 1, :], XBN) for dd in range(2)]
                T3n, hOn = lru_wave(chains, xsel)
                cp(nsb[:, col + cA:col + cA + 2], hOa[:, 0::2, 255], hOn, ["nsb"], arena=True)
                cp(nsb[:, col + 4 + cA:col + 4 + cA + 2], hOa[:, 1::2, 0], hOn, ["nsb"], arena=True)
                tt(T3a[:, 0:2, :], hOa[:, 0::2, :], hOa[:, 1::2, :], ALU.add, hOn, T3n, arena=True)
                tt(ylT[:, cA:cB + 1, :], T3a[:, 0:2, :], gel[:, cA:cB + 1, :], ALU.mult, T3n + [f"gel{hs}"], ["ylT"], arena=True)
        else:
            chains = [dict(cc=cc, d=0, rev=False, init=st_sb[:, 0, cc:cc + 1], init_names=["st"]) for cc in range(4)]
            T3n, hOn = lru_wave(chains, [(T2v, xba[:], XBN)])
            cp(st_sb[:, 0, :], hOa[:, :, 255], hOn, ["st"], arena=True)
            tt(T3a[:], hOa[:], HQ[:, :, j * 256:(j + 1) * 256], ALU.add, hOn + ["HQ"], T3n, arena=True)
            tt(ylT[:], T3a[:], gel[:], ALU.mult, T3n + [f"gel{hs}"], ["ylT"], arena=True)
    def tail_out_mm(slot0, hs):
        ypT = ypT2[hs]
        pend = []
        for t in range(2):
            s_ = slot0 + t
            b = bank2()
            for half in range(2):
                for kc in range(8):
                    lhs = ypT[:, kc, t * 128:(t + 1) * 128] if kc < 4 else ylT[:, kc - 4, t * 128:(t + 1) * 128]
                    mm(bap(b + half), lhs, w_out_sb[:, kc, half * 512:(half + 1) * 512], kc == 0, kc == 7,
                       [f"ypT{hs}", "ylT", "w_out"], [bn(b + half)], arena=True)
            pend.append((b, s_, post_norm_stats(b, s_)))
        return pend

    def tail_out_apply(pend, v, hs):
        for t, (b, s_, c) in enumerate(pend):
            post_norm_apply(b, s_, c, 0, v, res=xin[2 * hs + t][:], res_name=xin_n[2 * hs + t])

    def post_norm_stats(b, s):
        full = PS[b // 2][:, :]
        c = ss_ctr[0] % 16
        ss_ctr[0] += 1
        act(tmpr[s % 2][:], full, AF.Square, [bn(b), bn(b + 1)], [f"tmpr{s % 2}", f"ss{c}"], accum=ss[:, c:c + 1])
        ts(var[:, c:c + 1], ss[:, c:c + 1], 1.0 / D, EPS, ALU.mult, ALU.add, [f"ss{c}"], [f"var{c}"], eng="pool")
        tt(rstd[:, c:c + 1], var[:, c:c + 1], mhalf[:, 0:1], ALU.pow, [f"var{c}", "mhalf"], [f"rstd{c}"], eng="pool")
        return c

    def post_norm_apply(b, s, c, nrm, v, res=None, res_name=None):
        full = PS[b // 2][:, :]
        stt(full, full, rstd[:, c:c + 1], GGt[nrm][:], ALU.mult, ALU.mult,
            [bn(b), bn(b + 1), f"rstd{c}", f"GG{nrm}"], [bn(b), bn(b + 1)])
        if res is None:
            res, res_name = X1[:, s, :], f"X1_{s}"
        tt(X1[:, s, :], full, res, ALU.add, [bn(b), bn(b + 1), res_name, f"X1_{s}"], [f"X1_{s}"])

    wf_ctr = [0, 0]
    sc_in_v = sc_in.rearrange("(kc p) n -> p kc n", p=128)
    sc_out_v = sc_out.rearrange("(f p) n -> p f n", p=128)

    def ffn_norm(v, hs):
        tiles = [X1[:, s, :] for s in range(2 * hs, 2 * hs + 2)]
        names = [f"X1_{s}" for s in range(2 * hs, 2 * hs + 2)]
        norm_transpose(tiles, names, 1, v, 2, hsel=hs, evac="dve")

    def ffn_pair(v, dst, hook=None, mid=None, hook2=None, after_head=None):
        S.arena_switch()
        head = []
        for f in range(2):
            i = wf_ctr[0] % 2
            wf_ctr[0] += 1
            for gu in range(2):
                dma("sp", wfi[i][:, :, gu, :], sc_in_v[:, :, gu * DFF + f * 128:gu * DFF + (f + 1) * 128],
                    [f"sc_in{q}" for q in range(4)], [f"wfi{i}{gu}"])
            bg = bank()
            bu_ = bank()
            for gu, bk in ((0, bg), (1, bu_)):
                for kc in range(8):
                    mm(bap(bk)[:, 0:256], wfi[i][:, kc, gu, :], hT[:, kc, 0:256], kc == 0, kc == 7,
                       [f"wfi{i}{gu}", "hT0"], [bn(bk)])
            head.append((f, i, bg, bu_))
        if after_head is not None:
            after_head()
        for (f, i, bg, bu_) in head:
            for gu, bk in ((0, bg), (1, bu_)):
                for kc in range(8):
                    mm(bap(bk)[:, 256:512], wfi[i][:, kc, gu, :], hT[:, kc, 256:512], kc == 0, kc == 7,
                       [f"wfi{i}{gu}", "hT1"], [bn(bk)])
            act(sg[f % 2], bap(bg), AF.Silu, [bn(bg)], [f"sg{f % 2}"], arena=True)
            tt(actT[:, f, :], sg[f % 2], bap(bu_), ALU.mult, [f"sg{f % 2}", bn(bu_)], ["actT"], arena=True)
        for f in range(2, NF):
            if mid is not None:
                mid(f)
            i = wf_ctr[0] % 2
            wf_ctr[0] += 1
            for gu in range(2):
                dma("sp", wfi[i][:, :, gu, :], sc_in_v[:, :, gu * DFF + f * 128:gu * DFF + (f + 1) * 128],
                    [f"sc_in{q}" for q in range(4)], [f"wfi{i}{gu}"])
            bg = bank()
            for kc in range(8):
                mm(bap(bg), wfi[i][:, kc, 0, :], hT[:, kc, :], kc == 0, kc == 7, [f"wfi{i}0", "hT0", "hT1"], [bn(bg)])
            bu_ = bank()
            for kc in range(8):
                mm(bap(bu_), wfi[i][:, kc, 1, :], hT[:, kc, :], kc == 0, kc == 7, [f"wfi{i}1", "hT0", "hT1"], [bn(bu_)])
            act(sg[f % 2], bap(bg), AF.Silu, [bn(bg)], [f"sg{f % 2}"], arena=True)
            tt(actT[:, f, :], sg[f % 2], bap(bu_), ALU.mult, [f"sg{f % 2}", bn(bu_)], ["actT"], arena=True)
        if hook is not None:
            hook()
        for m in range(8):
            i = wf_ctr[1] % 2
            wf_ctr[1] += 1
            dma("sp", wfo[i][:], sc_out_v[:, :, m * 128:(m + 1) * 128], ["sc_out0", "sc_out1"], [f"wfo{i}"])
            b = bank()
            for f in range(NF):
                mm(bap(b), wfo[i][:, f, :], actT[:, f, :], f == 0, f == NF - 1, [f"wfo{i}", "actT"], [bn(b)], arena=True)
            cp(ffT[:, m, :], bap(b), [bn(b)], ["ffT"], eng="act", arena=True)
            if hook2 is not None:
                hook2(m)
        pend = []
        for t in range(4):
            b = bank2()
            for m in range(8):
                tr(PS[b // 2][:, m * 128:(m + 1) * 128], ffT[:, m, t * 128:(t + 1) * 128], ["ffT"], [bn(b + m // 4)],
                   arena=True)
            pend.append((b, t, post_norm_stats(b, t)))
        for (b, t, c) in pend:
            post_norm_apply(b, t, c, 1, v)
            dma("pool", dst[t * 128:(t + 1) * 128, :], X1[:, t, :], [f"X1_{t}"], [f"ydst{len(S.ops)}"], is_out=True)
        S.arena_switch()

    cp(st_sb[:], st0_sb[:], ["st0"], ["st"])
    for j in range(NQB - 1, -5, -1):
        q_iter(j, j + 3, j + 4)
    while deferred:
        deferred.pop(0)()
    S.arena_switch()
    blocks = [(xp[q * 256:(q + 1) * 256, :], 0, q, 0) for q in range(4)] + \
             [(xf[j * 256:(j + 1) * 256, :], 1, j, 1) for j in range(NSB)]
    dsts = [yp[0:512, :], yp[512:1024, :]] + [ys[q * 512:(q + 1) * 512, :] for q in range(4)]

    def prea(i):
        src, kind, j, v = blocks[i]
        pre_a(src, v, i % 2)

    def front_A(p):
        _, kind, j0, v = blocks[2 * p]
        mixer_front(kind, j0, v, 0)
        mixer_tail_a(0)

    prea(0)
    pre_b(0)
    prea(1)
    pre_b(1)
    front_A(0)
    for p in range(6):
        i0, i1 = 2 * p, 2 * p + 1
        _, kind, j0, v = blocks[i0]
        j1 = blocks[i1][2]
        mixer_front(kind, j1, v, 1)
        if p == 1:
            dma("pool", pm_one[:], pm_s.rearrange("g (sc p) t -> p g sc t", p=128), ["pm"], ["pm"])
        mixer_tail(kind, j0, 0, v, 0)
        pend0 = tail_out_mm(0, 0)
        mixer_tail_a(1)
        tail_out_apply(pend0, v, 0)
        mixer_tail(kind, j1, 2, v, 1)
        ffn_norm(v, 0)
        pend1 = tail_out_mm(2, 1)
        tail_out_apply(pend1, v, 1)

        def mid(f, p=p):
            if p < 5:
                if f == 3:
                    prea(2 * p + 2)
                elif f == 8:
                    prea(2 * p + 3)

        def hook(p=p):
            if p < 5:
                pre_b1(0)

        def hook2(m, p=p):
            if p < 5:
                if m == 0:
                    pre_b2(0)
                    pre_b1(1)
                elif m == 1:
                    pre_b2(1)
                elif m == 2:
                    front_A(p + 1)
        ffn_pair(v, dsts[p], hook=hook, mid=mid, hook2=hook2, after_head=lambda v=v: ffn_norm(v, 1))
        if p == 1:
            for nrm in range(2):
                dma("sp", GGt[nrm][:], gg_sc[nrm:nrm + 1, :].partition_broadcast(128), [f"ggsc{nrm}", f"GG{nrm}"], [f"GG{nrm}"])
        if p == 1:
            bt = bank()
            tr(bap(bt, 128)[0:32, :], nsb[:, 0:32], ["nsb"], [bn(bt)])
            cp(nsT[:], bap(bt, 128)[0:32, :], [bn(bt)], ["nsT"])
            dma("pool", ns, nsT[:], ["nsT"], ["nsdst"], is_out=True)

    S.finalize()
    with ExitStack() as es:
        esems = {e: es.enter_context(nc.semaphore(f"sem_{e}")) for e in Sched.ENGS}
        dsems = {}
        for q in ("sp", "pool"):
            for i in range(S.n_dma_sems):
                dsems[(q, i)] = es.enter_context(nc.semaphore(f"dsem_{q}_{i}"))
        block = es.enter_context(nc.Block())
        S.emit(nc, block, esems, dsems)
    return nc


def _pool_mats(L_rows, mirrored):
    wins = (2, 4, 8, 16)
    out = np.zeros((4, 256, 256), np.float32)
    for g, w in enumerate(wins):
        left = w // 2
        right = w - 1 - left
        A = np.zeros((L_rows, L_rows), np.float64)
        for t in range(L_rows):
            lo = max(t - left, 0)
            hi = min(t + right + 1, L_rows)
            A[t, lo:hi] = 1.0 / (hi - lo)
        A -= np.eye(L_rows)
        if mirrored:
            A = A[::-1, ::-1]
        for r in range(256 // L_rows):
            sl = slice(r * L_rows, (r + 1) * L_rows)
            out[g, sl, sl] = A.T
    return out


def _pp(vec, nchunk):
    return np.ascontiguousarray(np.asarray(vec, np.float32).reshape(nchunk, 128).T)


_NC_CACHE = {}


def kernel(x_prompt, x_sample, c, state_lru, c_ctx, w_mod, b_mod, norm_mix_pre, norm_mix_post,
           norm_ffn_pre, norm_ffn_post, w_in, w_pool, pool_scale, conv_w, conv_b,
           lru_w_a, lru_b_a, lru_w_x, lru_b_x, lru_lambda, w_out, w_ffn_in, w_ffn_out):
    f = lambda a: np.ascontiguousarray(np.asarray(a, np.float32))
    x_prompt, x_sample, c, state_lru, c_ctx = map(f, (x_prompt, x_sample, c, state_lru, c_ctx))
    shared = {
        "w_mod": f(w_mod[0]),
        "bmod_pp": _pp(b_mod[0], 48),
        "bmod_row": f(b_mod[0]).reshape(1, -1),
        "g_pp": np.ascontiguousarray(np.stack([_pp(norm_mix_pre[0], 8), _pp(norm_ffn_pre[0], 8)], axis=1)),
        "gpost_row": np.ascontiguousarray(np.stack([f(norm_mix_post[0]), f(norm_ffn_post[0])], axis=0)),
        "w_in": f(w_in[0]),
        "w_pool": f(w_pool[0]),
        "psc_pp": _pp(pool_scale[0], 4),
        "convb_pp": _pp(conv_b[0], 4),
        "w_out": f(w_out[0]),
        "w_ffn_in": f(w_ffn_in[0]),
        "w_ffn_out": f(w_ffn_out[0]),
        "ident": np.eye(128, dtype=np.float32),
    }
    cw = f(conv_w[0])
    zero = np.zeros((1, LW), np.float32)
    in_maps = []
    for core in range(8):
        b = core // 2
        mir = core % 2 == 1
        dP, dQ = (1, 0) if mir else (0, 1)
        xp = x_prompt[4 * core:4 * core + 4]
        xf = x_sample[b]
        if mir:
            xp = xp[:, ::-1]
            xf = xf[::-1]
            taps = np.concatenate([zero, cw[::-1]], axis=0)
        else:
            taps = np.concatenate([cw, zero], axis=0)
        m = dict(shared)
        m["xp"] = np.ascontiguousarray(xp.reshape(NPB * 256, D))
        m["xf"] = np.ascontiguousarray(xf)
        m["cvT"] = np.ascontiguousarray(np.stack([_pp(c_ctx, 8), _pp(c[b], 8)], axis=2))
        m["st0"] = np.ascontiguousarray(np.stack([_pp(state_lru[b, 0, dP], 4), _pp(state_lru[b, 0, dQ], 4)], axis=1))
        m["conv5_pp"] = np.ascontiguousarray(taps.T.reshape(4, 128, 5).transpose(1, 0, 2))
        m["gwa"] = np.ascontiguousarray(f(lru_w_a[0])[[dP, dQ]])
        m["gwx"] = np.ascontiguousarray(f(lru_w_x[0])[[dP, dQ]])
        m["gba_pp"] = np.ascontiguousarray(np.stack([_pp(lru_b_a[0, dP], 4), _pp(lru_b_a[0, dQ], 4)], axis=1))
        m["gbx_pp"] = np.ascontiguousarray(np.stack([_pp(lru_b_x[0, dP], 4), _pp(lru_b_x[0, dQ], 4)], axis=1))
        m["lam_pp"] = np.ascontiguousarray(np.stack([_pp(lru_lambda[0, dP], 4), _pp(lru_lambda[0, dQ], 4)], axis=1))
        m["pm_p"] = _pool_mats(256, mir)
        m["pm_s"] = _pool_mats(64, mir)
        in_maps.append(m)

    if "nc" not in _NC_CACHE:
        _NC_CACHE["nc"] = build_program()
    nc = _NC_CACHE["nc"]
    res = bu.run_bass_kernel_spmd(nc, in_maps, core_ids=list(range(8)))

    y_prompt = np.empty((32, 256, D), np.float32)
    y_sample = np.empty((4, 4096, D), np.float32)
    new_state = np.empty((32, 1, 2, LW), np.float32)
    for core in range(8):
        r = res.results[core]
        b = core // 2
        mir = core % 2 == 1
        dP, dQ = (1, 0) if mir else (0, 1)
        ypc = np.asarray(r["yp"], np.float32).reshape(4, 256, D)
        ysc = np.asarray(r["ys"], np.float32)
        nsc = np.asarray(r["ns"], np.float32).reshape(4, 2, 4 * 128)
        if mir:
            y_prompt[4 * core:4 * core + 4] = ypc[:, ::-1]
            y_sample[b, 2048:] = ysc[::-1]
        else:
            y_prompt[4 * core:4 * core + 4] = ypc
            y_sample[b, :2048] = ysc
        new_state[4 * core:4 * core + 4, 0, dP] = nsc[:, 0]
        new_state[4 * core:4 * core + 4, 0, dQ] = nsc[:, 1]
    return (y_prompt, y_sample, new_state)
```
